# Optimizing a Trainium2 kernel written in Bass

```python
import jax, jax.numpy as jnp
from jax import lax
import numpy as np

D_MODEL = 2048
BATCH = 2
SEQ = 4096
DEPTH = 1

D_MIX = D_MODEL
HG_DK = 128
HG_DV = 128
HG_WIDTH = D_MIX // 2
HG_HEADS = HG_WIDTH // HG_DV
HG_QK = HG_HEADS * HG_DK
GDN_DK = 128
GDN_DV = 128
GDN_WIDTH = D_MIX - HG_WIDTH
GDN_HEADS = GDN_WIDTH // GDN_DV
GDN_QK = GDN_HEADS * GDN_DK
CONV_K = 4
CHUNK = 64
D_FF = 4 * D_MODEL
N_MOD = 6
EPS = 1e-6

HG_COLS = 2 * HG_QK + 2 * HG_WIDTH
GDN_CONV_CH = 2 * GDN_QK + GDN_WIDTH
GDN_COLS = GDN_CONV_CH + GDN_WIDTH + 2 * GDN_HEADS
IN_COLS = HG_COLS + GDN_COLS

kernel_name = "hybrid_hgrn2_gdn_parallel_heads_adaln"


def rmsnorm(x, w):
    x32 = x.astype(jnp.float32)
    y = x32 * lax.rsqrt(jnp.mean(x32 * x32, axis=-1, keepdims=True) + EPS)
    return (y * w.astype(jnp.float32)).astype(x.dtype)


def l2norm(x):
    return x * lax.rsqrt(jnp.sum(x * x, axis=-1, keepdims=True) + EPS)


def to_chunks(x):
    B, T, H, D = x.shape
    return x.reshape(B, T // CHUNK, CHUNK, H, D).transpose(1, 0, 3, 2, 4)


def from_chunks(x):
    N, B, H, C, D = x.shape
    return x.transpose(1, 0, 3, 2, 4).reshape(B, N * C, H, D)


def causal_conv(u, w):
    ch = u.shape[-1]
    return lax.conv_general_dilated(
        u, w[:, None, :].astype(u.dtype), window_strides=(1,), padding=[(CONV_K - 1, 0)],
        dimension_numbers=('NWC', 'WIO', 'NWC'), feature_group_count=ch)


def hgrn2_chunked(q, log_f, k, v):
    B, T, H, DK = q.shape
    DV = v.shape[-1]
    causal = jnp.tril(jnp.ones((CHUNK, CHUNK), dtype=bool))

    def step(S, xs):
        qc, lfc, kc, vc = xs
        b = jnp.cumsum(lfc, axis=-2)
        diff = b[:, :, :, None, :] - b[:, :, None, :, :]
        decay = jnp.exp(jnp.where(causal[None, None, :, :, None], diff, -jnp.inf))
        attn = jnp.einsum('bhtk,bhsk,bhtsk->bhts', qc, kc, decay)
        o = (jnp.einsum('bhts,bhsv->bhtv', attn, vc)
             + jnp.einsum('bhtk,bhkv->bhtv', qc * jnp.exp(b), S))
        b_last = b[:, :, -1:, :]
        S_new = (S * jnp.exp(b_last[:, :, 0, :, None])
                 + jnp.einsum('bhsk,bhsv->bhkv', kc * jnp.exp(b_last - b), vc))
        return S_new, o

    S0 = jnp.zeros((B, H, DK, DV), jnp.float32)
    _, o = lax.scan(step, S0, (to_chunks(q), to_chunks(log_f), to_chunks(k), to_chunks(v)))
    return from_chunks(o)


def gated_delta_chunked(q, k, v, log_a, beta):
    B, T, H, DK = q.shape
    DV = v.shape[-1]
    qc, kc, vc = to_chunks(q), to_chunks(k), to_chunks(v)
    g = jnp.cumsum(to_chunks(log_a[..., None])[..., 0], axis=-1)
    bc = to_chunks(beta[..., None])[..., 0]
    incl = jnp.tril(jnp.ones((CHUNK, CHUNK), dtype=bool))
    strict = jnp.tril(jnp.ones((CHUNK, CHUNK), dtype=bool), -1)
    gamma = jnp.exp(jnp.where(incl, g[..., :, None] - g[..., None, :], -jnp.inf))
    kk = jnp.einsum('nbhtk,nbhsk->nbhts', kc, kc)
    m = jnp.where(strict, bc[..., :, None] * kk * gamma, 0.0)
    a_mat = jnp.eye(CHUNK, dtype=jnp.float32) + m
    rhs = jnp.concatenate([vc * bc[..., None], kc * (bc * jnp.exp(g))[..., None]], axis=-1)
    sol = lax.linalg.triangular_solve(a_mat, rhs, left_side=True, lower=True, unit_diagonal=True)
    u, w = sol[..., :DV], sol[..., DV:]
    qk = jnp.einsum('nbhtk,nbhsk->nbhts', qc, kc) * gamma
    q_dec = qc * jnp.exp(g)[..., None]
    k_tail = kc * jnp.exp(g[..., -1:] - g)[..., None]
    tail = jnp.exp(g[..., -1])

    def step(S, xs):
        u_c, w_c, qk_c, qd_c, kt_c, tl_c = xs
        v_new = u_c - jnp.einsum('bhck,bhkv->bhcv', w_c, S)
        o = jnp.einsum('bhck,bhkv->bhcv', qd_c, S) + jnp.einsum('bhts,bhsv->bhtv', qk_c, v_new)
        S = S * tl_c[..., None, None] + jnp.einsum('bhck,bhcv->bhkv', kt_c, v_new)
        return S, o

    S0 = jnp.zeros((B, H, DK, DV), jnp.float32)
    _, o = lax.scan(step, S0, (u, w, qk, q_dec, k_tail, tail))
    return from_chunks(o)


def hgrn2_group(p, lb, norm_w):
    B, T, _ = p.shape
    dt = p.dtype
    p32 = p.astype(jnp.float32)
    q = p32[..., :HG_QK].reshape(B, T, HG_HEADS, HG_DK)
    f_logit = p32[..., HG_QK:2 * HG_QK].reshape(B, T, HG_HEADS, HG_DK)
    i_in = p32[..., 2 * HG_QK:2 * HG_QK + HG_WIDTH].reshape(B, T, HG_HEADS, HG_DV)
    g_out = p[..., 2 * HG_QK + HG_WIDTH:].reshape(B, T, HG_HEADS, HG_DV)
    f = lb + (1.0 - lb) * jax.nn.sigmoid(f_logit)
    o = hgrn2_chunked(q, jnp.log(f), 1.0 - f, i_in).astype(dt)
    o = rmsnorm(o, norm_w) * jax.nn.silu(g_out)
    return o.reshape(B, T, HG_WIDTH)


def gdn_group(p, conv_w, a_log, dt_bias, norm_w):
    B, T, _ = p.shape
    dt = p.dtype
    qkv = jax.nn.silu(causal_conv(p[..., :GDN_CONV_CH], conv_w)).astype(jnp.float32)
    q = l2norm(qkv[..., :GDN_QK].reshape(B, T, GDN_HEADS, GDN_DK)) * (GDN_DK ** -0.5)
    k = l2norm(qkv[..., GDN_QK:2 * GDN_QK].reshape(B, T, GDN_HEADS, GDN_DK))
    v = qkv[..., 2 * GDN_QK:].reshape(B, T, GDN_HEADS, GDN_DV)
    off = GDN_CONV_CH
    g_out = p[..., off:off + GDN_WIDTH].reshape(B, T, GDN_HEADS, GDN_DV)
    a = p[..., off + GDN_WIDTH:off + GDN_WIDTH + GDN_HEADS].astype(jnp.float32)
    b = p[..., off + GDN_WIDTH + GDN_HEADS:].astype(jnp.float32)
    log_a = -jnp.exp(a_log.astype(jnp.float32)) * jax.nn.softplus(a + dt_bias.astype(jnp.float32))
    beta = jax.nn.sigmoid(b)
    o = gated_delta_chunked(q, k, v, log_a, beta).astype(dt)
    o = rmsnorm(o, norm_w) * jax.nn.silu(g_out)
    return o.reshape(B, T, GDN_WIDTH)


def setup_inputs(seed: int = 0) -> dict:
    key = jax.random.key(seed)
    ks = jax.random.split(key, 20)
    f32 = jnp.float32
    nrm = lambda k, s, sc: jax.random.normal(k, s, f32) * sc
    gain = lambda k, s: 1.0 + 0.05 * jax.random.normal(k, s, f32)
    dtv = jnp.exp(jax.random.uniform(ks[13], (DEPTH, GDN_HEADS), f32, np.log(1e-3), np.log(1e-1)))
    return {
        "x": nrm(ks[0], (BATCH, SEQ, D_MODEL), 1.0),
        "c": nrm(ks[1], (BATCH, D_MODEL), 1.0),
        "w_ada": nrm(ks[2], (DEPTH, D_MODEL, N_MOD * D_MODEL), 0.5 * D_MODEL ** -0.5),
        "b_ada": nrm(ks[3], (DEPTH, N_MOD * D_MODEL), 0.02),
        "pre_mix_norm": gain(ks[4], (DEPTH, D_MODEL)),
        "post_mix_norm": gain(ks[5], (DEPTH, D_MODEL)),
        "pre_ffn_norm": gain(ks[6], (DEPTH, D_MODEL)),
        "post_ffn_norm": gain(ks[7], (DEPTH, D_MODEL)),
        "w_in": nrm(ks[8], (DEPTH, D_MODEL, IN_COLS), D_MODEL ** -0.5),
        "hg_lb_logits": nrm(ks[9], (DEPTH + 1, HG_HEADS, HG_DK), 0.5),
        "hg_norm": gain(ks[10], (DEPTH, HG_DV)),
        "gdn_conv_w": nrm(ks[11], (DEPTH, CONV_K, GDN_CONV_CH), CONV_K ** -0.5),
        "gdn_a_log": jnp.log(jax.random.uniform(ks[12], (DEPTH, GDN_HEADS), f32, 1.0, 16.0)),
        "gdn_dt_bias": dtv + jnp.log(-jnp.expm1(-dtv)),
        "gdn_norm": gain(ks[14], (DEPTH, GDN_DV)),
        "w_out": nrm(ks[15], (DEPTH, D_MIX, D_MODEL), D_MIX ** -0.5),
        "w_ff1": nrm(ks[16], (DEPTH, D_MODEL, D_FF), D_MODEL ** -0.5),
        "w_ff2": nrm(ks[17], (DEPTH, D_FF, D_MODEL), D_FF ** -0.5),
    }


def reference(x, c, w_ada, b_ada, pre_mix_norm, post_mix_norm, pre_ffn_norm, post_ffn_norm,
              w_in, hg_lb_logits, hg_norm, gdn_conv_w, gdn_a_log, gdn_dt_bias, gdn_norm,
              w_out, w_ff1, w_ff2):
    lb_all = jnp.cumsum(jax.nn.softmax(hg_lb_logits.astype(jnp.float32), axis=0), axis=0)
    c_act = jax.nn.silu(c)
    for l in range(DEPTH):
        mod = c_act @ w_ada[l] + b_ada[l]
        sh_m, sc_m, gt_m, sh_f, sc_f, gt_f = jnp.split(mod[:, None, :], N_MOD, axis=-1)
        h = rmsnorm(x, pre_mix_norm[l]) * (1.0 + sc_m) + sh_m
        proj = h @ w_in[l]
        o_hg = hgrn2_group(proj[..., :HG_COLS], lb_all[l], hg_norm[l])
        o_gdn = gdn_group(proj[..., HG_COLS:], gdn_conv_w[l], gdn_a_log[l], gdn_dt_bias[l], gdn_norm[l])
        y = jnp.concatenate([o_hg, o_gdn], axis=-1) @ w_out[l]
        x = x + gt_m * rmsnorm(y, post_mix_norm[l])
        h = rmsnorm(x, pre_ffn_norm[l]) * (1.0 + sc_f) + sh_f
        y = jnp.square(jax.nn.relu(h @ w_ff1[l])) @ w_ff2[l]
        x = x + gt_f * rmsnorm(y, post_ffn_norm[l])
    return x
```

```python
import numpy as np
from contextlib import ExitStack
import concourse.bass as bass
import concourse.mybir as mybir
from concourse.bass_utils import run_bass_kernel_spmd

F32 = mybir.dt.float32
BF16 = mybir.dt.bfloat16
AF = mybir.ActivationFunctionType
ALU = mybir.AluOpType

COMPUTE = ("pe", "act", "dve", "pool")
ENGS = ("pe", "act", "dve", "pool", "sp")
NDSEM = 8

D = 2048
T = 4096
STW = 512
NST = T // STW
KC = D // 128
NCT = 17
ABW = 32
WCOLS = 16 * 128 + ABW


def coff(ct):
    return ct * 128 if ct <= 8 else 8 * 128 + ABW + (ct - 9) * 128


def cw(ct):
    return ABW if ct == 8 else 128
T2 = 1024
DFF = 8192
EPS = 1e-6
NEG = -1000.0
DEBUG = False
STOP = 0


class _Stop(Exception):
    pass
PHASE2 = True
NST1 = NST
SBUF_BYTES = 206 * 1024


class Op:
    __slots__ = ("eng", "fn", "waits", "sig", "owns", "is_dma")

    def __init__(self, eng, fn, is_dma):
        self.eng = eng
        self.fn = fn
        self.waits = []
        self.sig = None
        self.owns = False
        self.is_dma = is_dma


class Prog:
    def __init__(self):
        self.ops = {e: [] for e in ENGS}
        self.count = {e: 0 for e in COMPUTE}
        self.dcount = {e: 0 for e in ENGS}
        self.waited = {e: {} for e in ENGS}
        self.buf = {}
        self.pending_nosig = {e: [] for e in COMPUTE}
        self.last_sig = {e: None for e in COMPUTE}
        self.last_dma = {e: [] for e in ENGS}
        self.async_ops = []
        self.dma_hist = {}
        self.recording = None

    def _need(self, op, dep):
        if dep is None or dep is op:
            return
        if dep.eng == "pe" and op.eng == "pe" and not dep.is_dma and not op.is_dma:
            return
        assert dep.sig is not None, "dependency on an op whose signal is not yet defined"
        key, val = dep.sig
        w = self.waited[op.eng]
        if w.get(key, 0) >= val:
            return
        w[key] = val
        op.waits = [(k, v) for (k, v) in op.waits if k != key] + [(key, val)]

    def add(self, eng, fn, r=(), w=(), sig=True, dma=False, async_sem=None):
        if self.recording is not None:
            self.recording.append((eng, fn, tuple(r), tuple(w), sig, dma, async_sem))
            return None
        op = Op(eng, fn, dma)
        psr = [k for k in r if k.startswith("ps")]
        if psr:
            r = [k for k in r if not k.startswith("ps")]
            w = list(w) + [k for k in psr if k not in w]
        for k in r:
            st = self.buf.get(k)
            if st is not None:
                self._need(op, st["w"])
        for k in w:
            st = self.buf.get(k)
            if st is not None:
                self._need(op, st["w"])
                for rd in st["r"].values():
                    self._need(op, rd)
        if async_sem is not None:
            op.sig = (async_sem, 1)
            op.owns = True
            self.async_ops.append(op)
        elif dma:
            n = self.dcount[eng]
            self.dcount[eng] = n + 1
            hist = self.dma_hist.setdefault(eng, [])
            if n >= NDSEM:
                self._need(op, hist[n - NDSEM])
            hist.append(op)
            op.sig = ("d_%s_%d" % (eng, n % NDSEM), 16 * (n // NDSEM + 1))
            op.owns = True
            self.last_dma[eng] = (self.last_dma[eng] + [op])[-NDSEM:]
        elif sig:
            self.count[eng] += 1
            op.sig = ("c_" + eng, self.count[eng])
            op.owns = True
            for p in self.pending_nosig[eng]:
                p.sig = op.sig
            self.pending_nosig[eng] = []
            self.last_sig[eng] = op
        else:
            self.pending_nosig[eng].append(op)
        for k in r:
            st = self.buf.setdefault(k, {"w": None, "r": {}})
            st["r"][(eng, dma)] = op
        for k in w:
            self.buf[k] = {"w": op, "r": {}}
        self.ops[eng].append(op)
        return op

    def barrier(self):
        deps = []
        for e in COMPUTE:
            assert not self.pending_nosig[e]
            if self.last_sig[e] is not None:
                deps.append(self.last_sig[e])
        for e in ENGS:
            deps.extend(self.last_dma[e])
        deps.extend(self.async_ops)
        for e in ENGS:
            op = Op(e, lambda eng: eng.nop(), False)
            for d in deps:
                self._need(op, d)
            if e in COMPUTE:
                self.count[e] += 1
                op.sig = ("c_" + e, self.count[e])
                op.owns = True
                self.last_sig[e] = op
            self.ops[e].append(op)

    def emit(self, nc, stack):
        for e in COMPUTE:
            assert not self.pending_nosig[e], "trailing non-signalling op on " + e
        sems = {}
        for e in COMPUTE:
            sems["c_" + e] = stack.enter_context(nc.semaphore("c_" + e))
        final = {}
        for e in ENGS:
            n = self.dcount[e]
            if n:
                for i in range(NDSEM):
                    k = "d_%s_%d" % (e, i)
                    sems[k] = stack.enter_context(nc.semaphore(k))
                    cnt = (n - i + NDSEM - 1) // NDSEM
                    if cnt:
                        final[k] = 16 * cnt
        for op in self.async_ops:
            k = op.sig[0]
            sems[k] = stack.enter_context(nc.semaphore(k))
            final[k] = 1
        block = stack.enter_context(nc.Block())
        handles = {"pe": block.tensor, "act": block.scalar, "dve": block.vector,
                   "pool": block.gpsimd, "sp": block.sync}

        def make(e):
            def body(eng):
                for op in self.ops[e]:
                    for (k, v) in op.waits:
                        eng.wait_ge(sems[k], v)
                    ins = op.fn(eng)
                    if op.owns:
                        ins.then_inc(sems[op.sig[0]], 16 if op.is_dma else 1)
                if e == "sp":
                    for k, v in final.items():
                        eng.wait_ge(sems[k], v)
                    for ce in COMPUTE:
                        if self.count[ce]:
                            eng.wait_ge(sems["c_" + ce], self.count[ce])
            return body

        for e in ENGS:
            if self.ops[e] or e == "sp":
                handles[e](make(e))


def build_program():
    nc = bass.Bass("TRN2", target_bir_lowering=False)
    try:
        _build(nc)
    except _Stop:
        pass
    return nc


def _build(nc):
    P = Prog()

    def din(name, shape, dt=F32):
        return nc.dram_tensor(name, list(shape), dt, kind="ExternalInput").ap()

    xb_d = din("xb", [T, D])
    cT_d = din("cT", [128, KC])
    wada_d = din("wada", [D, 3072])
    bada_d = din("bada", [1, 3072])
    norms_d = din("norms", [4, D])
    win_d = din("win", [D, WCOLS])
    lbl_d = din("lbl", [128, 4])
    hgn_d = din("hgn", [128, 1])
    gdnn_d = din("gdnn", [128, 1])
    convw_d = din("convw", [128, 24])
    alog_d = din("alog", [1, 2])
    dtb_d = din("dtb", [1, 2])
    if PHASE2:
        wout_d = din("wout", [D, D])
        wff1_d = din("wff1", [D, DFF])
        wff2_d = din("wff2", [DFF, D])
        xtok_d = din("xtok", [T2, D])
    out_d = nc.dram_tensor("out", [T2, D], F32, kind="ExternalOutput").ap()
    dbg = {}
    if DEBUG:
        dbg["proj"] = nc.dram_tensor("dbg_proj", [NCT * 128, STW], F32, kind="ExternalOutput").ap()
        dbg["omix"] = nc.dram_tensor("dbg_omix", [512, T], BF16, kind="ExternalOutput").ap()
        dbg["x1"] = nc.dram_tensor("dbg_x1", [T2, D], F32, kind="ExternalOutput").ap()
        dbg["mod"] = nc.dram_tensor("dbg_mod", [4, 3072], F32, kind="ExternalOutput").ap()

    mod_in = nc.dram_tensor("mod_in", [1, 3072], F32)
    mod_all = nc.dram_tensor("mod_all", [4, 3072], F32)
    oT_loc = nc.dram_tensor("oT_loc", [4 * 512, T2], BF16)
    oT_all = nc.dram_tensor("oT_all", [4 * 4 * 512, T2], BF16)
    x1_d = nc.dram_tensor("x1_scr", [T2, D], F32)
    modflat = mod_all.ap().rearrange("r (o f) -> o (r f)", o=1)

    with ExitStack() as st:
        ARENA = SBUF_BYTES
        arena = st.enter_context(nc.sbuf_tensor("arena", [128, ARENA // 4], F32))
        pos = [0]

        def take(n, dt=F32):
            nbytes = n * (4 if dt == F32 else 2)
            nbytes = (nbytes + 63) // 64 * 64
            a = pos[0]
            assert a + nbytes <= ARENA, "SBUF arena overflow %d" % (a + nbytes)
            pos[0] = a + nbytes
            v = arena[:, a // 4:(a + nbytes) // 4]
            if dt != F32:
                v = v.bitcast(dt)
            return v[:, 0:n]

        psf = [st.enter_context(nc.psum_tensor("psf%d" % i, [128, 512], F32)) for i in range(8)]
        ccsem = [st.enter_context(nc.semaphore("ccsem%d" % i)) for i in range(2)]

        def pf(i):
            return psf[i][:]

        def pb(i):
            return psf[i][:].bitcast(BF16)

        def pk(bank, q0=0, q1=4):
            return ["ps%d" % bank]

        def MM(out, lhsT, rhs, start, stop, r, w, sig=None):
            P.add("pe", lambda e: e.matmul(out, lhsT=lhsT, rhs=rhs, start=start, stop=stop), r=r, w=w,
                  sig=(stop if sig is None else sig))

        def TR(out, in_, ident, r, w, sig=True):
            P.add("pe", lambda e: e.transpose(out=out, in_=in_, identity=ident), r=r, w=w, sig=sig)

        def ACT(out, in_, func, r, w, scale=1.0, bias=None, accum=None):
            def f(e):
                kw = {}
                if bias is not None:
                    kw["bias"] = bias
                if accum is not None:
                    kw["accum_out"] = accum
                return e.activation(out=out, in_=in_, func=func, scale=scale, **kw)
            P.add("act", f, r=r, w=w)

        def TT(eng, out, in0, in1, op, r, w):
            P.add(eng, lambda e: e.tensor_tensor(out=out, in0=in0, in1=in1, op=op), r=r, w=w)

        def TS(eng, out, in0, s1, s2, op0, op1, r, w):
            if s2 is None:
                P.add(eng, lambda e: e.tensor_single_scalar(out=out, in_=in0, scalar=s1, op=op0), r=r, w=w)
            else:
                P.add(eng, lambda e: e.tensor_scalar(out=out, in0=in0, scalar1=s1, scalar2=s2, op0=op0, op1=op1), r=r, w=w)

        def STT(eng, out, in0, scalar, in1, op0, op1, r, w):
            P.add(eng, lambda e: e.scalar_tensor_tensor(out=out, in0=in0, scalar=scalar, in1=in1, op0=op0, op1=op1), r=r, w=w)

        def CP(eng, out, in_, r, w):
            if eng == "act":
                ACT(out, in_, AF.Copy, r, w)
            else:
                P.add(eng, lambda e: e.tensor_copy(out=out, in_=in_), r=r, w=w)

        def MSET(eng, ap, val, w, r=()):
            P.add(eng, lambda e: e.memset(ap, val), r=r, w=w)

        def DMA(eng, out, in_, r, w):
            P.add(eng, lambda e: e.dma_start(out=out, in_=in_), r=r, w=w, dma=True)

        def ASEL(ap, pattern, cmp, fill, base, cm, key):
            P.add("pool", lambda e: e.affine_select(out=ap, in_=ap, pattern=pattern, compare_op=cmp, fill=fill,
                                                    base=base, channel_multiplier=cm), r=[key], w=[key])

        def v3(ap, b):
            return ap.rearrange("p (a b) -> p a b", b=b)

        def ckpt(n):
            if STOP == n:
                P.barrier()
                P.emit(nc, st)
                raise _Stop()

        ident_f = take(128)
        ident_b = take(128, BF16)
        ones_f = take(128)
        onesdiv_b = take(128, BF16)
        onescol_b = take(8, BF16)
        Ublk_f = take(128)
        Bblk_f = take(128)
        C0_f = take(128)
        C1_f = take(128)
        maskU_f = Ublk_f
        mneg_strict = take(128)
        mneg_inclT = take(128)
        rmask = take(STW)
        eps_col = take(1)
        one_col = take(1)
        lnsc_col = take(1)

        MSET("pool", ident_f, 0.0, ["ident_f"])
        ASEL(ident_f, [[-1, 128]], ALU.not_equal, 1.0, 0, 1, "ident_f")
        CP("pool", ident_b, ident_f, ["ident_f"], ["ident_b"])
        MSET("pool", ones_f, 1.0, ["ones_f"])
        MSET("pool", onesdiv_b, 1.0 / 128.0, ["onesdiv_b"])
        MSET("pool", onescol_b, 1.0, ["onescol_b"])
        MSET("pool", Ublk_f, 1.0, ["Ublk"])
        ASEL(Ublk_f, [[1, 128]], ALU.is_ge, 0.0, 0, -1, "Ublk")
        MSET("pool", Ublk_f[0:64, 64:128], 0.0, ["Ublk"], r=["Ublk"])
        MSET("pool", Bblk_f, 0.0, ["Bblk"])
        MSET("pool", Bblk_f[0:64, 0:64], 1.0, ["Bblk"], r=["Bblk"])
        MSET("pool", Bblk_f[64:128, 64:128], 1.0, ["Bblk"], r=["Bblk"])
        MSET("pool", C0_f, 0.0, ["C0"])
        MSET("pool", C0_f[0:64, :], 1.0, ["C0"], r=["C0"])
        MSET("pool", C1_f, 0.0, ["C1"])
        MSET("pool", C1_f[64:128, :], 1.0, ["C1"], r=["C1"])
        MSET("pool", mneg_strict, 0.0, ["mns"])
        ASEL(mneg_strict, [[-1, 128]], ALU.is_gt, NEG, 0, 1, "mns")
        MSET("pool", mneg_strict[64:128, 0:64], NEG, ["mns"], r=["mns"])
        MSET("pool", mneg_inclT, 0.0, ["mni"])
        ASEL(mneg_inclT, [[1, 128]], ALU.is_ge, NEG, 0, -1, "mni")
        MSET("pool", mneg_inclT[0:64, 64:128], NEG, ["mni"], r=["mni"])
        MSET("pool", rmask, 1.0, ["rmask"])
        MSET("pool", v3(rmask, 64)[:, :, 0:1], 0.0, ["rmask"], r=["rmask"])
        maskU4 = take(STW)
        ident4_b = take(STW, BF16)
        for pr in range(4):
            CP("pool", maskU4[:, pr * 128:(pr + 1) * 128], Ublk_f, ["Ublk"], ["maskU4"])
            CP("pool", ident4_b[:, pr * 128:(pr + 1) * 128], ident_f, ["ident_f"], ["ident4_b"])
        MSET("pool", eps_col, EPS, ["eps_col"])
        MSET("pool", one_col, 1.0, ["one_col"])
        MSET("pool", lnsc_col, float(np.log(128.0 ** -0.5)), ["lnsc_col"])

        lbl = take(4)
        hgn = take(1)
        gdnn = take(1)
        convw = take(24)
        alog_b = take(2)
        dtb_b = take(2)
        DMA("sp", lbl, lbl_d[:, :], [], ["lbl"])
        DMA("sp", hgn, hgn_d[:, :], [], ["hgn"])
        DMA("sp", gdnn, gdnn_d[:, :], [], ["gdnn"])
        DMA("sp", convw, convw_d[:, :], [], ["convw"])
        DMA("sp", alog_b, alog_d[0:1, :].partition_broadcast(128), [], ["alog_b"])
        DMA("sp", dtb_b, dtb_d[0:1, :].partition_broadcast(128), [], ["dtb_b"])

        lbt = take(2)
        hg_sc = take(2)
        hg_bi = take(2)
        hg_nsc = take(2)
        TT("dve", lbt, lbl[:, 0:2], lbl[:, 2:4], ALU.subtract, ["lbl"], ["lbt"])
        ACT(lbt, lbt, AF.Tanh, ["lbt"], ["lbt"], scale=0.5)
        TS("dve", hg_sc, lbt, -0.25, 0.25, ALU.mult, ALU.add, ["lbt"], ["hg_sc"])
        TS("dve", hg_bi, lbt, 0.25, 0.75, ALU.mult, ALU.add, ["lbt"], ["hg_bi"])
        TS("dve", hg_nsc, lbt, 0.25, -0.25, ALU.mult, ALU.add, ["lbt"], ["hg_nsc"])
        nA8 = take(8)
        dtb8 = take(8)
        nA2 = take(2)
        ACT(nA2, alog_b, AF.Exp, ["alog_b"], ["nA2"])
        TS("dve", nA2, nA2, -1.0, None, ALU.mult, None, ["nA2"], ["nA2"])
        for pr in range(4):
            CP("dve", nA8[:, 2 * pr:2 * pr + 2], nA2, ["nA2"], ["nA8"])
            CP("dve", dtb8[:, 2 * pr:2 * pr + 2], dtb_b, ["dtb_b"], ["dtb8"])
        dg = take(24 * 128, BF16)
        for i in range(24):
            TS("pool", dg[:, i * 128:(i + 1) * 128], ident_f, convw[:, i:i + 1], None, ALU.mult, None,
               ["ident_f", "convw"], ["dg"])

        ssx = take(2)
        rsx = take(2)
        wmodT = take(KC)
        shT_b = take(KC, BF16)
        pbias = take(NCT)
        pbh = take(2)
        wmod2T = take(KC)
        shfT_b = take(KC, BF16)
        shfT_f = take(KC)
        r16 = take(3 * 128)
        mT = take(3 * KC)
        const_end = pos[0]
        wi = take(KC * WCOLS, BF16)
        wi3 = v3(wi, WCOLS)
        wi_end = pos[0]
        pos[0] = const_end
        woutb = take(KC * D, BF16)
        wout3 = v3(woutb, D)
        pos[0] = wi_end
        hT = take(KC * STW, BF16)
        hT3 = v3(hT, STW)
        xt = [take(D), take(D)]
        hb = take(D, BF16)
        for kc in range(KC):
            DMA("pool", wi3[:, kc, :], win_d[kc * 128:(kc + 1) * 128, :], [], ["wi"])
        base_pos = pos[0]
        ckpt(1)

        cT = take(KC)
        cact = take(KC)
        bada = take(3072)
        modsb = take(3072)
        wa = [take(3072), take(3072)]
        DMA("sp", cT, cT_d[:, :], [], ["cT"])
        DMA("sp", bada[0:1, :], bada_d[0:1, :], [], ["bada"])
        ACT(cact, cT, AF.Silu, ["cT"], ["cact"])
        for kc in range(KC):
            DMA("sp" if kc % 2 == 0 else "act", wa[kc % 2], wada_d[kc * 128:(kc + 1) * 128, :], [], ["wa%d" % (kc % 2)])
            for nt in range(6):
                MM(pf(nt)[0:1, :], cact[:, kc:kc + 1], wa[kc % 2][:, nt * 512:(nt + 1) * 512], kc == 0, kc == KC - 1,
                   ["cact", "wa%d" % (kc % 2)], pk(nt), sig=(nt == 5 or kc == KC - 1))
        for nt in range(6):
            TT("dve", modsb[0:1, nt * 512:(nt + 1) * 512], pf(nt)[0:1, :], bada[0:1, nt * 512:(nt + 1) * 512], ALU.add,
               pk(nt) + ["bada"], ["modsb"])
        DMA("sp", mod_in.ap()[0:1, :], modsb[0:1, :], ["modsb"], ["mod_in"])

        def cc0(e):
            ins = e.collective_compute("AllGather", ALU.bypass, replica_groups=[[0, 1, 2, 3], [4, 5, 6, 7]],
                                       ins=[mod_in.ap().opt()], outs=[mod_all.ap().opt()])
            ins.then_inc(ccsem[0], 1)
            e.wait_ge(ccsem[0], 1)
            return e.nop()
        P.add("pool", cc0, r=["mod_in"], w=["mod_all"])
        if DEBUG:
            DMA("pool", dbg["mod"][:, :], mod_all.ap()[:, :], ["mod_all"], [])
        P.barrier()
        pos[0] = base_pos
        ckpt(2)

        def modT(off_sh, off_sc, nrow, outw, outs_b, tag, outs_f=None):
            r3 = v3(r16, 128)
            DMA("sp", r3[0:16, 0, :], modflat[0:1, off_sh:off_sh + D].rearrange("o (c p) -> (o c) p", p=128), ["mod_all"], ["r16"])
            DMA("sp", r3[0:16, 1, :], modflat[0:1, off_sc:off_sc + D].rearrange("o (c p) -> (o c) p", p=128), ["mod_all"], ["r16"])
            DMA("sp", r3[0:16, 2, :], norms_d[nrow:nrow + 1, :].rearrange("o (c p) -> (o c) p", p=128), [], ["r16"])
            for i in range(3):
                TR(pf(1)[:, i * KC:(i + 1) * KC], r3[0:16, i, :], ident_f[0:16, 0:16], ["r16", "ident_f"], pk(1, 0, 1), sig=(i == 2))
            CP("dve", mT, pf(1)[:, 0:3 * KC], pk(1, 0, 1), ["mT"])
            CP("dve", outs_b, mT[:, 0:KC], ["mT"], [tag + "_s"])
            if outs_f is not None:
                CP("dve", outs_f, mT[:, 0:KC], ["mT"], [tag + "_sf"])
            STT("dve", outw, mT[:, KC:2 * KC], 1.0, mT[:, 2 * KC:3 * KC], ALU.add, ALU.mult, ["mT"], [tag + "_w"])

        modT(0, D, 0, wmodT, shT_b, "m1")
        if PHASE2:
            modT(3 * D, 4 * D, 2, wmod2T, shfT_b, "m2", shfT_f)
        ckpt(3)

        tk = take
        th = [tk(STW), tk(STW)]
        qraw = [tk(STW), tk(STW)]
        vT = [tk(STW, BF16), tk(STW, BF16)]
        sg = [tk(STW, BF16) for _ in range(4)]
        abrows = tk(STW)
        ug = [[tk(STW + 4, BF16) for _ in range(3)] for _ in range(2)]
        lf = tk(STW)
        kk1 = tk(STW)
        bcs = tk(STW)
        eb = tk(STW)
        enb = tk(STW)
        d2 = tk(STW)
        qh = tk(STW, BF16)
        kt = tk(STW, BF16)
        khT = tk(STW, BF16)
        vtok = tk(STW, BF16)
        khtok = tk(STW, BF16)
        attnm = tk(STW, BF16)
        S32 = [tk(128) for _ in range(4)]
        Sb_hg = [[tk(128, BF16), tk(128, BF16)] for _ in range(2)]
        Sb_gd = [[tk(128, BF16), tk(128, BF16)] for _ in range(2)]
        sqo2 = [tk(STW, BF16), tk(STW, BF16)]
        rso2 = [tk(STW), tk(STW)]
        omix = [tk(STW, BF16), tk(STW, BF16)]
        xs_k = tk(STW, BF16)
        xs_q = tk(STW, BF16)
        xs_v = tk(STW, BF16)
        sqs = tk(STW, BF16)
        abc = tk(16)
        sc = {n: tk(8) for n in ("apd", "e1", "la", "e2", "lnbn", "beta")}
        gsc = tk(32)
        lnr = tk(16)
        hsc = [{n: tk(4) for n in ("aM", "ncS", "aQ", "L3", "skbg", "sktail", "t0", "t1", "tmp")} for _ in range(2)]
        kbg = [tk(STW, BF16), tk(STW, BF16)]
        ktl = [tk(STW, BF16), tk(STW, BF16)]
        vbt = [tk(STW, BF16), tk(STW, BF16)]
        dgc = tk(128)
        dgq = tk(128)
        EN = tk(STW, BF16)
        EQ = tk(STW, BF16)
        qsc = tk(STW, BF16)
        qdec = [tk(STW, BF16), tk(STW, BF16)]
        QKT = [tk(STW, BF16), tk(STW, BF16)]
        Nb = [tk(STW, BF16), tk(STW, BF16)]
        Pb = [tk(STW, BF16), tk(STW, BF16)]
        Rb = [[tk(STW, BF16), tk(STW, BF16)] for _ in range(2)]
        nwT = [tk(STW, BF16), tk(STW, BF16)]
        vnew = [tk(STW, BF16), tk(STW, BF16)]
        p1_end = pos[0]

        for h in range(2):
            MSET("pool", vnew[h], 0.0, ["vnew%d" % h])
        for h in range(4):
            MSET("pool", S32[h], 0.0, ["S32_%d" % h])
        for h in range(2):
            MSET("pool", Sb_hg[h][0], 0.0, ["Sbh%d_0" % h])
            MSET("pool", Sb_gd[h][0], 0.0, ["Sbg%d_0" % h])
            for x in range(3):
                MSET("pool", ug[h][x][:, 0:4], 0.0, ["ug%d%d" % (h, x)])

        PX = 0
        PP = (2, 3)
        PT = 4
        PA, PB_, PC, PD = 5, 6, 7, 1

        def stageA(stt, i):
            tt = stt * 4 + i
            xb_ = xt[tt % 2]
            xk = "xt%d" % (tt % 2)
            hbk = "hb"
            DMA("sp", xb_, xb_d[tt * 128:(tt + 1) * 128, :], [], [xk])
            ACT(hb, xb_, AF.Square, [xk], [hbk, "ssx"], accum=ssx[:, 0:1])
            TS("dve", ssx[:, 1:2], ssx[:, 0:1], 1.0 / D, EPS, ALU.mult, ALU.add, ["ssx"], ["ssx1"])
            ACT(rsx[:, 0:1], ssx[:, 1:2], AF.Ln, ["ssx1"], ["rsx0"])
            ACT(rsx[:, 1:2], rsx[:, 0:1], AF.Exp, ["rsx0"], ["rsx1"], scale=-0.5)
            ACT(hb, xb_, AF.Copy, [xk, "rsx1"], [hbk], scale=rsx[:, 1:2])
            for half in range(2):
                for c8 in range(8):
                    c = half * 8 + c8
                    TR(pb(PX)[:, c8 * 128:(c8 + 1) * 128], hb[:, c * 128:(c + 1) * 128], ident_b,
                       [hbk, "ident_b"], pk(PX), sig=(c8 == 7))
                TT("dve", hT3[:, half * 8:(half + 1) * 8, i * 128:(i + 1) * 128], v3(pb(PX), 128),
                   wmodT[:, half * 8:(half + 1) * 8].unsqueeze(2).to_broadcast([128, 8, 128]), ALU.mult,
                   pk(PX) + ["m1_w"], ["hT"])

        def proj_tile(stt, ct):
            bank = PP[ct % 2]
            for kc in range(KC):
                MM(pf(bank)[0:cw(ct), :], wi3[:, kc, coff(ct):coff(ct) + cw(ct)], hT3[:, kc, :], kc == 0, kc == KC - 1,
                   ["wi", "hT"], pk(bank))
            return pf(bank), pk(bank), pbias[:, ct:ct + 1]

        def EV(eng, out, ps, k, bcol, w):
            if eng == "act":
                ACT(out, ps, AF.Identity, k + ["pbias"], w, bias=bcol)
            else:
                TS(eng, out, ps, bcol, None, ALU.add, None, k + ["pbias"], w)

        def onorm(stt, ps_o, ps_o_key, ncol, ncol_key, sgt, sgk, row0, ty, sbank):
            t0 = stt * STW
            sqo, rso = sqo2[ty], rso2[ty]
            sqk, rsk = "sqo%d" % ty, "rso%d" % ty
            ACT(sqo, ps_o, AF.Square, ps_o_key, [sqk])
            MM(pf(sbank), onesdiv_b, sqo, True, True, ["onesdiv_b", sqk], pk(sbank))
            ACT(rso, pf(sbank), AF.Ln, pk(sbank) + ["eps_col"], [rsk], bias=eps_col[:, 0:1])
            ACT(rso, rso, AF.Exp, [rsk], [rsk], scale=-0.5)
            STT("dve", rso, ps_o, ncol, rso, ALU.mult, ALU.mult, ps_o_key + [rsk, ncol_key], [rsk])
            om = omix[ty]
            omk = "omix%d" % ty
            TT("pool", om, rso, sgt, ALU.mult, [rsk, sgk], [omk])
            jj, toff = t0 // T2, t0 % T2
            DMA("sp", oT_loc.ap()[jj * 512 + row0:jj * 512 + row0 + 128, toff:toff + STW], om, [omk], ["oT_loc%d" % jj])
            if DEBUG:
                DMA("sp", dbg["omix"][row0:row0 + 128, t0:t0 + STW], om, [omk], [])

        def proj_all(stt):
            for h in range(2):
                ps, k, bc = proj_tile(stt, h)
                ACT(th[h], ps, AF.Tanh, k + ["pbh"], ["th%d" % h], scale=0.5, bias=pbh[:, h:h + 1])
                ps, k, bc = proj_tile(stt, 2 + h)
                EV("dve", qraw[h], ps, k, bc, ["qraw%d" % h])
                yield
            for h in range(2):
                ps, k, bc = proj_tile(stt, 4 + h)
                EV("act", vT[h], ps, k, bc, ["vT%d" % h])
                ps, k, bc = proj_tile(stt, 6 + h)
                ACT(sg[h], ps, AF.Silu, k + ["pbias"], ["sg%d" % h], bias=bc)
                yield
            ps, k, bc = proj_tile(stt, 8)
            TS("dve", abrows[0:4, :], ps[0:4, :], pbias[0:4, 8:9], None, ALU.add, None, k + ["pbias"], ["abrows"])
            gdn_scalars(stt)
            for h in range(2):
                for x in range(3):
                    ps, k, bc = proj_tile(stt, 9 + x * 2 + h)
                    EV("act" if x != 1 else "dve", ug[h][x][:, 4:4 + STW], ps, k, bc, ["ug%d%d" % (h, x)])
                    yield
                if h == 0:
                    proj_gate(stt, h)

        def proj_gate(stt, h):
            ps, k, bc = proj_tile(stt, 15 + h)
            ACT(sg[2 + h], ps, AF.Silu, k + ["pbias"], ["sg%d" % (2 + h)], bias=bc)

        def hg_head(stt, h):
            HT, HA, HC = 2, 3, 4
            thh, qrh, vTh = th[h], qraw[h], vT[h]
            thk_, qrk_, vTk_ = "th%d" % h, "qraw%d" % h, "vT%d" % h
            ckpt(61)
            ACT(lf, thh, AF.Ln, [thk_, "hg_sc", "hg_bi"], ["lf"], scale=hg_sc[:, h:h + 1], bias=hg_bi[:, h:h + 1])
            TS("dve", kk1, thh, hg_nsc[:, h:h + 1], hg_sc[:, h:h + 1], ALU.mult, ALU.add, [thk_, "hg_nsc", "hg_sc"], ["kk1"])
            P.add("dve", lambda e: e.tensor_tensor_scan(out=bcs, data0=rmask, data1=lf, initial=0.0,
                                                         op0=ALU.mult, op1=ALU.add), r=["rmask", "lf"], w=["bcs"])
            ACT(eb, bcs, AF.Exp, ["bcs"], ["eb"])
            ACT(enb, bcs, AF.Exp, ["bcs"], ["enb"], scale=-1.0)
            b3 = v3(bcs, 64)
            STT("dve", v3(d2, 64), b3, -1.0, b3[:, :, 63:64].to_broadcast([128, 8, 64]), ALU.mult, ALU.add, ["bcs"], ["d2"])
            ACT(d2, d2, AF.Exp, ["d2"], ["d2"])
            yield
            TT("dve", qh, qrh, eb, ALU.mult, [qrk_, "eb"], ["qh"])
            TT("dve", kt, kk1, enb, ALU.mult, ["kk1", "enb"], ["kt"])
            TT("pool", khT, kk1, d2, ALU.mult, ["kk1", "d2"], ["khT"])
            yield
            ckpt(62)
            for pr in range(4):
                TR(pb(HT)[:, pr * 128:(pr + 1) * 128], vTh[:, pr * 128:(pr + 1) * 128], ident_b, [vTk_, "ident_b"], pk(HT), sig=(pr == 3))
            CP("act", vtok, pb(HT)[:, 0:512], pk(HT), ["vtok"])
            yield
            for pr in range(4):
                TR(pb(HA)[:, pr * 128:(pr + 1) * 128], khT[:, pr * 128:(pr + 1) * 128], ident_b, ["khT", "ident_b"], pk(HA), sig=(pr == 3))
            CP("dve", khtok, pb(HA)[:, 0:512], pk(HA), ["khtok"])
            yield
            ckpt(63)
            for pr in range(4):
                MM(pf(HA)[:, pr * 128:(pr + 1) * 128], kt[:, pr * 128:(pr + 1) * 128], qh[:, pr * 128:(pr + 1) * 128], True, True,
                   ["kt", "qh"], pk(HA), sig=(pr == 3))
            TT("dve", attnm, pf(HA), maskU4, ALU.mult, pk(HA) + ["maskU4"], ["attnm"])
            yield
            ckpt(64)
            Sk = "S32_%d" % h
            for par_, bank in ((0, HT), (1, HA)):
                for pr in range(4):
                    rows = slice(par_ * 64, par_ * 64 + 64)
                    MM(pf(bank)[:, pr * 128:(pr + 1) * 128], khtok[rows, pr * 128:(pr + 1) * 128], vtok[rows, pr * 128:(pr + 1) * 128],
                       True, True, ["khtok", "vtok"], pk(bank), sig=(pr == 3))
            for c in range(8):
                pr, hf = c // 2, c % 2
                rows = slice(hf * 64, hf * 64 + 64)
                cols = slice(c * 64, c * 64 + 64)
                gi = stt * 8 + c
                sap, skey = Sb_hg[h][gi % 2], "Sbh%d_%d" % (h, gi % 2)
                nap, nkey = Sb_hg[h][(gi + 1) % 2], "Sbh%d_%d" % (h, (gi + 1) % 2)
                dbank = HT if hf == 0 else HA
                MM(pf(HC)[:, cols], sap, qh[:, cols], True, False, [skey, "qh"], pk(HC))
                MM(pf(HC)[:, cols], vtok[:, pr * 128:(pr + 1) * 128], attnm[:, pr * 128 + hf * 64:pr * 128 + hf * 64 + 64], False, True,
                   ["vtok", "attnm"], pk(HC), sig=True)
                STT("dve", nap, S32[h], eb[:, c * 64 + 63:c * 64 + 64], pf(dbank)[:, pr * 128:(pr + 1) * 128], ALU.mult, ALU.add,
                    [Sk, "eb"] + pk(dbank), [nkey])
                STT("dve", S32[h], S32[h], eb[:, c * 64 + 63:c * 64 + 64], pf(dbank)[:, pr * 128:(pr + 1) * 128], ALU.mult, ALU.add,
                    [Sk, "eb"] + pk(dbank), [Sk])
                yield
            ckpt(65)
            onorm(stt, pf(HC), pk(HC), hgn[:, 0:1], "hgn", sg[h], "sg%d" % h, h * 128, 0, HA)

        def gdn_scalars(stt):
            for pr in range(4):
                TR(pf(0)[:, pr * 4:pr * 4 + 4], abrows[0:4, pr * 128:(pr + 1) * 128], ident_f[0:4, 0:4], ["abrows", "ident_f"], pk(0), sig=(pr == 3))
            CP("dve", abc, pf(0)[:, 0:16], pk(0), ["abc"])
            a3 = v3(abc, 4)
            TT("dve", v3(sc["apd"], 2), a3[:, :, 0:2], v3(dtb8, 2), ALU.add, ["abc", "dtb8"], ["apd"])
            ACT(sc["e1"], sc["apd"], AF.Exp, ["apd"], ["e1"])
            ACT(sc["e1"], sc["e1"], AF.Ln, ["e1", "one_col"], ["e1"], bias=one_col[:, 0:1])
            TT("dve", sc["la"], sc["e1"], nA8, ALU.mult, ["e1", "nA8"], ["la"])
            ACT(v3(sc["e2"], 2), a3[:, :, 2:4], AF.Exp, ["abc"], ["e2"], scale=-1.0)
            ACT(sc["lnbn"], sc["e2"], AF.Ln, ["e2", "one_col"], ["lnbn"], bias=one_col[:, 0:1])
            ACT(sc["beta"], sc["lnbn"], AF.Exp, ["lnbn"], ["beta"], scale=-1.0)
            for kind, L, Lk in ((0, Ublk_f, "Ublk"), (1, Bblk_f, "Bblk"), (2, C0_f, "C0"), (3, C1_f, "C1")):
                for pr in range(4):
                    MM(pf(0)[:, 32 + kind * 8 + pr * 2:32 + kind * 8 + pr * 2 + 2], L, sc["la"][:, pr * 2:pr * 2 + 2], True, True,
                       [Lk, "la"], pk(0), sig=(kind == 3 and pr == 3))
            CP("dve", gsc, pf(0)[:, 32:64], pk(0), ["gsc"])

        GX, GY, GT = 5, 6, 0
        GO, GV = 7, 1

        def gdn_pre(stt, h):
            hs = hsc[h]
            hk = lambda n: "hsc%d_%s" % (h, n)
            ukeys = ["ug%d%d" % (h, x) for x in range(3)]
            xs = (xs_k, xs_q, xs_v)
            xkeys = ("xsk", "xsq", "xsv")
            kbg_, ktl_, vbt_, qdec_, QKT_, nwT_ = kbg[h], ktl[h], vbt[h], qdec[h], QKT[h], nwT[h]
            kbk, ktk, vbk, qdk, qkk, nwk = ("kbg%d" % h, "ktl%d" % h, "vbt%d" % h, "qdec%d" % h, "QKT%d" % h, "nwT%d" % h)
            Rb_ = Rb[h]
            Rbk = lambda i: "Rb%d_%d" % (h, i)
            for x in range(3):
                ti = x * 2 + h
                for tap in range(4):
                    MM(pf(GX), dg[:, (ti * 4 + tap) * 128:(ti * 4 + tap + 1) * 128], ug[h][x][:, 1 + tap:1 + tap + STW], tap == 0, tap == 3,
                       ["dg", ukeys[x]], pk(GX))
                ACT(xs[x], pf(GX), AF.Silu, pk(GX), [xkeys[x]])
                CP("pool", ug[h][x][:, 1:4], ug[h][x][:, STW + 1:STW + 4], [ukeys[x]], [ukeys[x]])
                yield
            for w_, src, sk in ((0, xs_k, xkeys[0]), (1, xs_q, xkeys[1])):
                TT("pool", sqs, src, src, ALU.mult, [sk], ["sqs"])
                for pr in range(4):
                    MM(pf(GY)[:, 64 + h * 8 + w_ * 4 + pr:64 + h * 8 + w_ * 4 + pr + 1], sqs[:, pr * 128:(pr + 1) * 128], onescol_b[:, 0:1], True, True,
                       ["sqs", "onescol_b"], pk(GY), sig=(pr == 3))
            lr = lnr[:, h * 8:h * 8 + 8]
            lrk = "lnr%d" % h
            ACT(lr, pf(GY)[:, 64 + h * 8:64 + h * 8 + 8], AF.Ln, pk(GY) + ["eps_col"], [lrk], bias=eps_col[:, 0:1])
            TS("dve", lr, lr, -0.5, None, ALU.mult, None, [lrk], [lrk])
            lnrk, lnrq = lr[:, 0:4], lr[:, 4:8]
            g3 = gsc.rearrange("p (k a b) -> p k a b", k=4, b=2)
            g_, gl_, c0_, c1_ = g3[:, 0, :, h], g3[:, 1, :, h], g3[:, 2, :, h], g3[:, 3, :, h]
            lnbn_h = v3(sc["lnbn"], 2)[:, :, h]
            beta_h = v3(sc["beta"], 2)[:, :, h]
            TT("dve", hs["ncS"], lnrk, g_, ALU.subtract, [lrk, "gsc"], [hk("ncS")])
            TT("dve", hs["tmp"], lnrk, g_, ALU.add, [lrk, "gsc"], [hk("tmp")])
            TT("dve", hs["aM"], hs["tmp"], lnbn_h, ALU.subtract, [hk("tmp"), "lnbn"], [hk("aM")])
            STT("dve", hs["aQ"], lnrq, lnsc_col[:, 0:1], g_, ALU.add, ALU.add, [lrk, "gsc", "lnsc_col"], [hk("aQ")])
            TT("dve", hs["L3"], hs["ncS"], gl_, ALU.add, [hk("ncS"), "gsc"], [hk("L3")])
            ACT(hs["skbg"], hs["aM"], AF.Exp, [hk("aM")], [hk("skbg")])
            ACT(hs["sktail"], hs["L3"], AF.Exp, [hk("L3")], [hk("sktail")])
            ACT(hs["t0"], c0_, AF.Exp, ["gsc"], [hk("t0")])
            ACT(hs["t1"], c1_, AF.Exp, ["gsc"], [hk("t1")])
            yield
            for pr in range(4):
                TR(pb(GY)[:, pr * 128:(pr + 1) * 128], xs_k[:, pr * 128:(pr + 1) * 128], ident_b, [xkeys[0], "ident_b"], pk(GY), sig=(pr == 3))
            TT("dve", v3(kbg_, 128), v3(pb(GY)[:, 0:512], 128), hs["skbg"].unsqueeze(2).to_broadcast([128, 4, 128]), ALU.mult,
               pk(GY) + [hk("skbg")], [kbk])
            TT("dve", v3(ktl_, 128), v3(pb(GY)[:, 0:512], 128), hs["sktail"].unsqueeze(2).to_broadcast([128, 4, 128]), ALU.mult,
               pk(GY) + [hk("sktail")], [ktk])
            for pr in range(4):
                TR(pb(GX)[:, pr * 128:(pr + 1) * 128], xs_v[:, pr * 128:(pr + 1) * 128], ident_b, [xkeys[2], "ident_b"], pk(GX), sig=(pr == 3))
            TT("dve", v3(vbt_, 128), v3(pb(GX)[:, 0:512], 128), beta_h.unsqueeze(2).to_broadcast([128, 4, 128]), ALU.mult,
               pk(GX) + ["beta"], [vbk])
            yield
            for pr in range(4):
                sl = slice(pr * 128, (pr + 1) * 128)
                TS("dve", dgc, ident_f, hs["ncS"][:, pr:pr + 1], None, ALU.mult, None, ["ident_f", hk("ncS")], ["dgc"])
                ACT(dgq, ident_f, AF.Copy, ["ident_f", hk("aQ")], ["dgq"], scale=hs["aQ"][:, pr:pr + 1])
                MM(pf(GX)[:, 0:128], ones_f, dgc, True, False, ["ones_f", "dgc"], pk(GX))
                MM(pf(GX)[:, 0:128], ident_f, mneg_strict, False, True, ["ident_f", "mns"], pk(GX))
                ACT(EN[:, sl], pf(GX)[:, 0:128], AF.Exp, pk(GX) + [hk("aM")], ["EN"], bias=hs["aM"][:, pr:pr + 1])
                MM(pf(GY)[:, 0:128], ones_f, dgq, True, True, ["ones_f", "dgq"], pk(GY))
                ACT(qsc[:, sl], pf(GY)[:, 0:128], AF.Exp, pk(GY), ["qsc"])
                MM(pf(GX)[:, 0:128], ones_f, dgq, True, False, ["ones_f", "dgq"], pk(GX))
                MM(pf(GX)[:, 0:128], ident_f, mneg_inclT, False, True, ["ident_f", "mni"], pk(GX))
                ACT(EQ[:, sl], pf(GX)[:, 0:128], AF.Exp, pk(GX) + [hk("ncS")], ["EQ"], bias=hs["ncS"][:, pr:pr + 1])
                yield
            for pr in range(4):
                sl = slice(pr * 128, (pr + 1) * 128)
                MM(pf(GY)[:, sl], xs_k[:, sl], xs_k[:, sl], True, True, [xkeys[0]], pk(GY), sig=(pr == 3))
            STT("dve", Nb[0], pf(GY), -1.0, EN, ALU.mult, ALU.mult, pk(GY) + ["EN"], ["Nb0"])
            for pr in range(4):
                sl = slice(pr * 128, (pr + 1) * 128)
                MM(pf(GX)[:, sl], xs_k[:, sl], xs_q[:, sl], True, True, [xkeys[0], xkeys[1]], pk(GX), sig=(pr == 3))
            TT("dve", QKT_, pf(GX), EQ, ALU.mult, pk(GX) + ["EQ"], [qkk])
            TT("pool", qdec_, xs_q, qsc, ALU.mult, [xkeys[1], "qsc"], [qdk])
            yield
            for pr in range(4):
                sl = slice(pr * 128, (pr + 1) * 128)
                TR(pb(GY)[:, sl], Nb[0][:, sl], ident_b, ["Nb0", "ident_b"], pk(GY), sig=(pr == 3))
            CP("act", Pb[0], pb(GY)[:, 0:512], pk(GY), ["Pb0"])
            TT("pool", Rb_[0], Pb[0], ident4_b, ALU.add, ["Pb0", "ident4_b"], [Rbk(0)])
            for j in range(1, 6):
                a, b_ = (j - 1) % 2, j % 2
                for pr in range(4):
                    sl = slice(pr * 128, (pr + 1) * 128)
                    MM(pf(GX)[:, sl], Pb[a][:, sl], Nb[a][:, sl], True, True, ["Pb%d" % a, "Nb%d" % a], pk(GX), sig=(pr == 3))
                yield
                CP("act", Nb[b_], pf(GX), pk(GX), ["Nb%d" % b_])
                if j < 5:
                    for pr in range(4):
                        sl = slice(pr * 128, (pr + 1) * 128)
                        MM(pf(GY)[:, sl], Nb[a][:, sl], Pb[a][:, sl], True, True, ["Pb%d" % a, "Nb%d" % a], pk(GY), sig=(pr == 3))
                    yield
                    CP("dve", Pb[b_], pf(GY), pk(GY), ["Pb%d" % b_])
                yield
                for pr in range(4):
                    sl = slice(pr * 128, (pr + 1) * 128)
                    MM(pf(GX)[:, sl], Nb[b_][:, sl], Rb_[a][:, sl], True, False, ["Nb%d" % b_, Rbk(a)], pk(GX))
                    MM(pf(GX)[:, sl], ident_b, Rb_[a][:, sl], False, True, ["ident_b", Rbk(a)], pk(GX), sig=(pr == 3))
                CP("act" if j % 2 else "dve", Rb_[b_], pf(GX), pk(GX), [Rbk(b_)])
                yield
            for pr in range(4):
                sl = slice(pr * 128, (pr + 1) * 128)
                MM(pf(GY)[:, sl], kbg_[:, sl], Rb_[1][:, sl], True, True, [kbk, Rbk(1)], pk(GY), sig=(pr == 3))
            ACT(nwT_, pf(GY), AF.Copy, pk(GY), [nwk], scale=-1.0)
            yield

        def gdn_chain(stt, h):
            hs = hsc[h]
            hk = lambda n: "hsc%d_%s" % (h, n)
            ktl_, vbt_, qdec_, QKT_, nwT_, vnew_ = ktl[h], vbt[h], qdec[h], QKT[h], nwT[h], vnew[h]
            ktk, vbk, qdk, qkk, nwk, vnk = ("ktl%d" % h, "vbt%d" % h, "qdec%d" % h, "QKT%d" % h, "nwT%d" % h, "vnew%d" % h)
            R = Rb[h][1]
            Rk = "Rb%d_1" % h
            Sk = "S32_%d" % (2 + h)
            S = S32[2 + h]
            for c in range(8):
                pr, hf = c // 2, c % 2
                sl = slice(pr * 128, (pr + 1) * 128)
                rows = slice(hf * 64, hf * 64 + 64)
                cols = slice(c * 64, c * 64 + 64)
                gi = stt * 8 + c
                Sb = Sb_gd[h][gi % 2]
                Sbk = "Sbg%d_%d" % (h, gi % 2)
                Sbn = Sb_gd[h][(gi + 1) % 2]
                Sbnk = "Sbg%d_%d" % (h, (gi + 1) % 2)
                psV = pf(GV)[:, 0:128]
                psS = pf(GV)[:, 128:256]
                MM(psV, R[:, sl], vbt_[:, sl], True, False, [Rk, vbk], pk(GV))
                MM(psV, nwT_[:, sl], Sb, False, True, [nwk, Sbk], pk(GV))
                yield
                yield
                CP("dve", vnew_[rows, sl], psV[rows, :], pk(GV), [vnk])
                yield
                yield
                MM(pf(GO)[:, cols], Sb, qdec_[:, cols], True, False, [Sbk, qdk], pk(GO))
                MM(pf(GO)[:, cols], vnew_[rows, sl], QKT_[rows, pr * 128 + hf * 64:pr * 128 + hf * 64 + 64], False, True,
                   [vnk, qkk], pk(GO), sig=True)
                MM(psS, ktl_[rows, sl], vnew_[rows, sl], True, True, [ktk, vnk], pk(GV))
                yield
                yield
                tl = hs["t0"] if hf == 0 else hs["t1"]
                STT("dve", Sbn, S, tl[:, pr:pr + 1], psS, ALU.mult, ALU.add, [Sk, hk("t0"), hk("t1")] + pk(GV), [Sbnk])
                STT("dve", S, S, tl[:, pr:pr + 1], psS, ALU.mult, ALU.add, [Sk, hk("t0"), hk("t1")] + pk(GV), [Sk])
                yield
            onorm(stt, pf(GO), pk(GO), gdnn[:, 0:1], "gdnn", sg[2 + h], "sg%d" % (2 + h), (2 + h) * 128, 1, GV)
            yield

        def exchange(jj):
            def cc(e):
                return e.collective_compute("AllGather", ALU.bypass, replica_groups=[[0, 1, 2, 3], [4, 5, 6, 7]],
                                            ins=[oT_loc.ap()[jj * 512:(jj + 1) * 512, :].opt()],
                                            outs=[oT_all.ap()[jj * 2048:(jj + 1) * 2048, :].opt()])
            P.add("pool", cc, r=["oT_loc%d" % jj], w=["oT_all%d" % jj], async_sem="cc_o%d" % jj)

        def interleave(*gens):
            gens = list(gens)
            while gens:
                for g_ in list(gens):
                    try:
                        next(g_)
                    except StopIteration:
                        gens.remove(g_)

        def run(g_):
            for _ in g_:
                pass

        def record(g_):
            P.recording = []
            run(g_)
            ops, P.recording = P.recording, None
            return ops

        def merge(*streams):
            idx = [0] * len(streams)
            for _ in range(sum(len(s_) for s_ in streams)):
                best = min((i for i in range(len(streams)) if idx[i] < len(streams[i])),
                           key=lambda i: (idx[i] + 0.5) / len(streams[i]))
                P.add(*streams[best][idx[best]])
                idx[best] += 1

        def record_calls(*fns):
            P.recording = []
            for f_ in fns:
                f_()
            ops, P.recording = P.recording, None
            return ops

        def pbias_block():
            for ct in range(NCT):
                for kc in range(KC):
                    MM(pf(1)[0:cw(ct), 64 + ct:64 + ct + 1], wi3[:, kc, coff(ct):coff(ct) + cw(ct)], shT_b[:, kc:kc + 1], kc == 0, kc == KC - 1,
                       ["wi", "m1_s"], pk(1, 0, 1), sig=(kc == KC - 1 and ct == NCT - 1))
            MSET("pool", pbias, 0.0, ["pbias"])
            CP("dve", pbias[:, 0:8], pf(1)[:, 64:72], pk(1, 0, 1) + ["pbias"], ["pbias"])
            CP("dve", pbias[0:ABW, 8:9], pf(1)[0:ABW, 72:73], pk(1, 0, 1) + ["pbias"], ["pbias"])
            CP("dve", pbias[:, 9:NCT], pf(1)[:, 73:64 + NCT], pk(1, 0, 1) + ["pbias"], ["pbias"])
            TS("dve", pbh, pbias[:, 0:2], 0.5, None, ALU.mult, None, ["pbias"], ["pbh"])

        merge(record_calls(pbias_block), record_calls(*[(lambda i_=i_: stageA(0, i_)) for i_ in range(4)]))
        run(proj_all(0))
        proj_gate(0, 1)
        for stt in range(NST1):
            nxt = stt + 1 < NST1
            sa1 = record_calls(lambda: stageA(stt + 1, 0), lambda: stageA(stt + 1, 1)) if nxt else []
            sa2 = record_calls(lambda: stageA(stt + 1, 2), lambda: stageA(stt + 1, 3)) if nxt else []
            merge(record(gdn_pre(stt, 0)), record(hg_head(stt, 0)), *([sa1] if nxt else []))
            if PHASE2 and stt == NST - 1:
                for c in range(KC):
                    DMA("pool", wout3[:, c, :], wout_d[c * 128:(c + 1) * 128, :], [], ["wi", "woutb"])
            if nxt:
                merge(record(gdn_chain(stt, 0)), record(gdn_pre(stt, 1)), record(hg_head(stt, 1)), sa2)
                merge(record(gdn_chain(stt, 1)), record(proj_all(stt + 1)))
                proj_gate(stt + 1, 1)
            else:
                merge(record(gdn_chain(stt, 0)), record(gdn_pre(stt, 1)))
                merge(record(gdn_chain(stt, 1)), record(hg_head(stt, 1)))
            if stt % 2 == 1 or stt == NST1 - 1:
                exchange(stt // 2)
        P.barrier()

        if not PHASE2:
            DMA("sp", out_d[0:128, :], xt[0], ["xt0"], [])
            P.emit(nc, st)
            return nc
        pos[0] = wi_end
        h2T = take(KC * T2, BF16)
        h2T3 = v3(h2T, T2)
        pb2 = take(64)
        p2mark = pos[0]
        oTs = take(KC * T2, BF16)
        oT3 = v3(oTs, T2)
        A1 = take(D)
        tmpn = take(D)
        xq = [take(D), take(D)]
        t2 = take(512)
        hb2 = [take(D, BF16), take(D, BF16)]
        st2 = take(16)
        h2tmp = take(1024)

        def ld_o(e, r_):
            pid = nc.partition_id([mybir.EngineType.Pool])
            j = pid % 4
            return e.dma_start(out=oT3[:, r_ * 4:(r_ + 1) * 4, :],
                               in_=oT_all.ap()[bass.ds(j * 2048 + r_ * 512, 512), :].rearrange("(b p) t -> p b t", p=128))
        for r_ in range(4):
            P.add("pool", (lambda c_: (lambda e: ld_o(e, c_)))(r_), r=["oT_all%d" % q_ for q_ in range(4)], w=["oTs"], dma=True)
        DMA("sp", A1, modflat[0:1, 2 * D:3 * D].partition_broadcast(128), ["mod_all"], ["A1"])
        DMA("sp", tmpn, norms_d[1:2, :].partition_broadcast(128), [], ["tmpn"])
        TT("dve", A1, A1, tmpn, ALU.mult, ["A1", "tmpn"], ["A1"])

        def rstd_from(ps_banks, keys, outcol, tag, junk):
            for nt in range(4):
                ACT(junk[:, nt * 512:(nt + 1) * 512], ps_banks[nt], AF.Square, keys[nt], ["junkq", "st2_%d" % nt],
                    accum=st2[:, nt:nt + 1])
            TT("dve", st2[:, 4:5], st2[:, 0:1], st2[:, 1:2], ALU.add, ["st2_0", "st2_1"], ["st2_4"])
            TT("dve", st2[:, 5:6], st2[:, 2:3], st2[:, 3:4], ALU.add, ["st2_2", "st2_3"], ["st2_5"])
            TT("dve", st2[:, 4:5], st2[:, 4:5], st2[:, 5:6], ALU.add, ["st2_4", "st2_5"], ["st2_4"])
            TS("dve", st2[:, 6:7], st2[:, 4:5], 1.0 / D, EPS, ALU.mult, ALU.add, ["st2_4"], ["st2_6"])
            ACT(st2[:, 7:8], st2[:, 6:7], AF.Ln, ["st2_6"], ["st2_7"])
            ACT(outcol, st2[:, 7:8], AF.Exp, ["st2_7"], [tag], scale=-0.5)

        def p22_mm(tt):
            xq_ = xq[tt % 2]
            xk = "xq%d" % (tt % 2)
            hbk = "hb2_%d" % (tt % 2)
            hbb = hb2[tt % 2]
            DMA("sp", xq_, xtok_d[tt * 128:(tt + 1) * 128, :], [], [xk])
            for nt in range(4):
                for c in range(KC):
                    MM(pf(nt), oT3[:, c, tt * 128:(tt + 1) * 128], wout3[:, c, nt * 512:(nt + 1) * 512], c == 0, c == KC - 1,
                       ["oTs", "woutb"], pk(nt))

        def p22_norm(tt):
            xq_ = xq[tt % 2]
            xk = "xq%d" % (tt % 2)
            hbk = "hb2_%d" % (tt % 2)
            hbb = hb2[tt % 2]
            for nt in range(4):
                ACT(hbb[:, nt * 512:(nt + 1) * 512], pf(nt), AF.Square, pk(nt), [hbk, "st2_%d" % nt], accum=st2[:, nt:nt + 1])
            TT("dve", st2[:, 4:5], st2[:, 0:1], st2[:, 1:2], ALU.add, ["st2_0", "st2_1"], ["st2_4"])
            TT("dve", st2[:, 5:6], st2[:, 2:3], st2[:, 3:4], ALU.add, ["st2_2", "st2_3"], ["st2_5"])
            TT("dve", st2[:, 4:5], st2[:, 4:5], st2[:, 5:6], ALU.add, ["st2_4", "st2_5"], ["st2_4"])
            TS("dve", st2[:, 6:7], st2[:, 4:5], 1.0 / D, EPS, ALU.mult, ALU.add, ["st2_4"], ["st2_6"])
            ACT(st2[:, 7:8], st2[:, 6:7], AF.Ln, ["st2_6"], ["st2_7"])
            ACT(st2[:, 8:9], st2[:, 7:8], AF.Exp, ["st2_7"], ["rs_y"], scale=-0.5)
            for nt in range(4):
                sl = slice(nt * 512, (nt + 1) * 512)
                STT("dve", t2, pf(nt), st2[:, 8:9], A1[:, sl], ALU.mult, ALU.mult, pk(nt) + ["rs_y", "A1"], ["t2"])
                TT("pool", xq_[:, sl], xq_[:, sl], t2, ALU.add, [xk, "t2"], [xk])
            DMA("sp", x1_d.ap()[tt * 128:(tt + 1) * 128, :], xq_, [xk], ["x1_d"])
            if DEBUG:
                DMA("sp", dbg["x1"][tt * 128:(tt + 1) * 128, :], xq_, [xk], [])
            ACT(hbb, xq_, AF.Square, [xk], [hbk, "st2_9"], accum=st2[:, 9:10])
            TS("dve", st2[:, 10:11], st2[:, 9:10], 1.0 / D, EPS, ALU.mult, ALU.add, ["st2_9"], ["st2_10"])
            ACT(st2[:, 11:12], st2[:, 10:11], AF.Ln, ["st2_10"], ["st2_11"])
            ACT(st2[:, 12:13], st2[:, 11:12], AF.Exp, ["st2_11"], ["rs_x1"], scale=-0.5)
            ACT(hbb, xq_, AF.Copy, [xk, "rs_x1"], [hbk], scale=st2[:, 12:13])

        def p22_tr(tt):
            xq_ = xq[tt % 2]
            xk = "xq%d" % (tt % 2)
            hbk = "hb2_%d" % (tt % 2)
            hbb = hb2[tt % 2]
            for half in range(2):
                for c8 in range(8):
                    c = half * 8 + c8
                    TR(pb(4 + half)[:, c8 * 128:(c8 + 1) * 128], hbb[:, c * 128:(c + 1) * 128], ident_b,
                       [hbk, "ident_b"], pk(4 + half), sig=(c8 == 7))
                TT("dve", v3(h2tmp, 128), v3(pb(4 + half), 128),
                   wmod2T[:, half * 8:(half + 1) * 8].unsqueeze(2).to_broadcast([128, 8, 128]), ALU.mult,
                   pk(4 + half) + ["m2_w"], ["h2tmp"])
                TT("dve", h2T3[:, half * 8:(half + 1) * 8, tt * 128:(tt + 1) * 128], v3(h2tmp, 128),
                   shfT_f[:, half * 8:(half + 1) * 8].unsqueeze(2).to_broadcast([128, 8, 128]), ALU.add,
                   ["h2tmp", "m2_sf"], ["h2T"])

        p22_mm(0)
        for tt in range(8):
            p22_norm(tt)
            if tt + 1 < 8:
                p22_mm(tt + 1)
            p22_tr(tt)
        P.barrier()

        pos[0] = const_end
        acc = take(8 * D)
        acc3 = v3(acc, D)
        pos[0] = p2mark
        G = 4
        NG = 64 // G
        w1r = [take(KC * 128, BF16) for _ in range(2 * G)]
        w2r = [take(G * 512, BF16) for _ in range(4)]
        uT = [take(T2, BF16) for _ in range(2 * G)]
        rl = [take(512), take(512)]
        A2 = take(D)
        x1b = take(D)
        junk3 = take(D, BF16)
        st3 = take(8)
        p23_end = pos[0]
        for g in range(NG):
            for f in range(G):
                fb = g * G + f
                ri = fb % (2 * G)
                DMA("pool", v3(w1r[ri], 128), wff1_d[:, fb * 128:(fb + 1) * 128].rearrange("(c p) f -> p c f", p=128),
                    [], ["w1r%d" % ri])
            for nt in range(4):
                DMA("pool", v3(w2r[nt], 512), wff2_d[g * G * 128:(g + 1) * G * 128, nt * 512:(nt + 1) * 512].rearrange("(f p) n -> p f n", p=128),
                    [], ["w2r%d" % nt])
            for f in range(G):
                fb = g * G + f
                ri = fb % (2 * G)
                for half in range(2):
                    bank = 4 + (fb % 2) * 2 + half
                    for c in range(KC):
                        MM(pf(bank), v3(w1r[ri], 128)[:, c, :], h2T3[:, c, half * 512:(half + 1) * 512], c == 0, c == KC - 1,
                           ["w1r%d" % ri, "h2T"], pk(bank))
                    ACT(rl[half], pf(bank), AF.Relu, pk(bank), ["rl%d" % half])
                    TT("dve" if half == 0 else "pool", uT[ri][:, half * 512:(half + 1) * 512], rl[half], rl[half], ALU.mult,
                       ["rl%d" % half], ["uT%d" % ri])
            last = (g == NG - 1)
            if last:
                DMA("sp", A2, modflat[0:1, 5 * D:6 * D].partition_broadcast(128), ["mod_all"], ["A2"])
                DMA("sp", x1b, norms_d[3:4, :].partition_broadcast(128), [], ["x1b"])
                TT("dve", A2, A2, x1b, ALU.mult, ["A2", "x1b"], ["A2"])
            order = [(nt, tt) for tt in range(8) for nt in range(4)] if last else [(nt, tt) for nt in range(4) for tt in range(8)]
            for i_, (nt, tt) in enumerate(order):
                bank = 1 + i_ % 3
                for f in range(G):
                    ri = (g * G + f) % (2 * G)
                    MM(pf(bank), uT[ri][:, tt * 128:(tt + 1) * 128], v3(w2r[nt], 512)[:, f, :], f == 0, f == G - 1,
                       ["uT%d" % ri, "w2r%d" % nt], pk(bank))
                dst = acc3[:, tt, nt * 512:(nt + 1) * 512]
                ak = "acc%d_%d" % (tt, nt)
                if g == 0:
                    CP("dve", dst, pf(bank), pk(bank), [ak])
                else:
                    TT("dve", dst, pf(bank), dst, ALU.add, pk(bank) + [ak], [ak])
                if last and nt == 3:
                    aks = ["acc%d_%d" % (tt, n_) for n_ in range(4)]
                    DMA("sp", x1b, x1_d.ap()[tt * 128:(tt + 1) * 128, :], ["x1_d"], ["x1b"])
                    ACT(junk3, acc3[:, tt, :], AF.Square, aks, ["junk3", "st3_0"], accum=st3[:, 0:1])
                    TS("dve", st3[:, 1:2], st3[:, 0:1], 1.0 / D, EPS, ALU.mult, ALU.add, ["st3_0"], ["st3_1"])
                    ACT(st3[:, 2:3], st3[:, 1:2], AF.Ln, ["st3_1"], ["st3_2"])
                    ACT(st3[:, 3:4], st3[:, 2:3], AF.Exp, ["st3_2"], ["st3_3"], scale=-0.5)
                    STT("dve", acc3[:, tt, :], acc3[:, tt, :], st3[:, 3:4], A2, ALU.mult, ALU.mult, aks + ["st3_3", "A2"], aks)
                    TT("pool", x1b, x1b, acc3[:, tt, :], ALU.add, ["x1b"] + aks, ["x1b"])
                    DMA("sp", out_d[tt * 128:(tt + 1) * 128, :], x1b, ["x1b"], [])

        print("SBUF use: phase1 %d, phase2.3 %d" % (p1_end, p23_end))
        P.emit(nc, st)
    return nc


_NC_CACHE = {}


def _core_inputs(inp, core):
    b, g = core // 4, core % 4
    f32 = np.float32
    x = inp["x"]
    w_in = inp["w_in"][0]
    hh = [2 * g, 2 * g + 1]
    cols = []
    HGQ, GOFF = 1024, 4096
    def hcols(base, h):
        return np.arange(base + h * 128, base + (h + 1) * 128)
    for base in (HGQ, 0, 2 * HGQ, 3 * HGQ):
        for h in hh:
            cols.append(hcols(base, h))
    abcols = np.array([GOFF + 4096 + hh[0], GOFF + 4096 + hh[1], GOFF + 4096 + 8 + hh[0], GOFF + 4096 + 8 + hh[1]])
    win = np.zeros((D, WCOLS), f32)
    ci = 0
    for cset in cols:
        win[:, coff(ci):coff(ci) + 128] = w_in[:, cset]
        ci += 1
    win[:, coff(ci):coff(ci) + 4] = w_in[:, abcols]
    ci += 1
    for x_, base in enumerate((GOFF + 1024, GOFF, GOFF + 2048)):
        for hi_, h in enumerate(hh):
            ct_ = 9 + 2 * x_ + hi_
            win[:, coff(ct_):coff(ct_) + 128] = w_in[:, hcols(base, h)]
    for hi_, h in enumerate(hh):
        win[:, coff(15 + hi_):coff(15 + hi_) + 128] = w_in[:, hcols(GOFF + 3072, h)]
    lbl = np.zeros((128, 4), f32)
    for layer in range(2):
        for hi, h in enumerate(hh):
            lbl[:, layer * 2 + hi] = inp["hg_lb_logits"][layer, h, :]
    cw = inp["gdn_conv_w"][0]
    convw = np.zeros((128, 24), f32)
    for xi, base in enumerate((1024, 0, 2048)):
        for hi, h in enumerate(hh):
            ti = xi * 2 + hi
            for tap in range(4):
                convw[:, ti * 4 + tap] = cw[tap, base + h * 128: base + (h + 1) * 128]
    return {
        "xb": np.ascontiguousarray(x[b]),
        "xtok": np.ascontiguousarray(x[b, g * T2:(g + 1) * T2]),
        "cT": np.ascontiguousarray(inp["c"][b].reshape(KC, 128).T),
        "wada": np.ascontiguousarray(inp["w_ada"][0][:, g * 3072:(g + 1) * 3072]),
        "bada": np.ascontiguousarray(inp["b_ada"][0][None, g * 3072:(g + 1) * 3072]),
        "win": win,
        "lbl": lbl,
        "convw": convw,
        "alog": np.ascontiguousarray(inp["gdn_a_log"][0][None, hh]),
        "dtb": np.ascontiguousarray(inp["gdn_dt_bias"][0][None, hh]),
    }


def kernel(**inputs):
    inp = {k: np.asarray(v) for k, v in inputs.items()}
    if "nc" not in _NC_CACHE:
        _NC_CACHE["nc"] = build_program()
    nc = _NC_CACHE["nc"]
    f32 = np.float32
    norms = np.ascontiguousarray(np.stack([inp["pre_mix_norm"][0], inp["post_mix_norm"][0],
                                           inp["pre_ffn_norm"][0], inp["post_ffn_norm"][0]]).astype(f32))
    perm = []
    for r in range(4):
        for base in (0, 1024):
            for h in (2 * r, 2 * r + 1):
                perm.append(np.arange(base + h * 128, base + (h + 1) * 128))
    perm = np.concatenate(perm)
    wout = np.ascontiguousarray(inp["w_out"][0][perm])
    shared = {
        "norms": norms,
        "hgn": np.ascontiguousarray(inp["hg_norm"][0][:, None]),
        "gdnn": np.ascontiguousarray(inp["gdn_norm"][0][:, None]),
        "wout": wout,
        "wff1": np.ascontiguousarray(inp["w_ff1"][0]),
        "wff2": np.ascontiguousarray(inp["w_ff2"][0]),
    }
    in_maps = []
    for core in range(8):
        m = _core_inputs(inp, core)
        m.update(shared)
        if not PHASE2:
            for k_ in ("wout", "wff1", "wff2", "xtok"):
                m.pop(k_)
        in_maps.append(m)
    res = run_bass_kernel_spmd(nc, in_maps, core_ids=list(range(8)))
    kernel.last_results = res
    out = np.empty((2, T, D), f32)
    for core in range(8):
        b, g = core // 4, core % 4
        out[b, g * T2:(g + 1) * T2] = res.results[core]["out"]
    return out
```

```python
import numpy as np
from contextlib import ExitStack
import concourse.bass as bass
import concourse.mybir as mybir
from concourse.bass_utils import run_bass_kernel_spmd

F32 = mybir.dt.float32
BF16 = mybir.dt.bfloat16
AF = mybir.ActivationFunctionType
ALU = mybir.AluOpType

COMPUTE = ("pe", "act", "dve", "pool")
ENGS = ("pe", "act", "dve", "pool", "sp")
NDSEM = 8

D = 2048
T = 4096
STW = 512
NST = T // STW
KC = D // 128
NCT = 17
ABW = 32
WCOLS = 16 * 128 + ABW


def coff(ct):
    return ct * 128 if ct <= 8 else 8 * 128 + ABW + (ct - 9) * 128


def cw(ct):
    return ABW if ct == 8 else 128
T2 = 1024
DFF = 8192
EPS = 1e-6
NEG = -1000.0
DEBUG = False
STOP = 0


class _Stop(Exception):
    pass
PHASE2 = True
NST1 = NST
SBUF_BYTES = 206 * 1024


class Op:
    __slots__ = ("eng", "fn", "waits", "sig", "owns", "is_dma")

    def __init__(self, eng, fn, is_dma):
        self.eng = eng
        self.fn = fn
        self.waits = []
        self.sig = None
        self.owns = False
        self.is_dma = is_dma


class Prog:
    def __init__(self):
        self.ops = {e: [] for e in ENGS}
        self.count = {e: 0 for e in COMPUTE}
        self.dcount = {e: 0 for e in ENGS}
        self.waited = {e: {} for e in ENGS}
        self.buf = {}
        self.pending_nosig = {e: [] for e in COMPUTE}
        self.last_sig = {e: None for e in COMPUTE}
        self.last_dma = {e: [] for e in ENGS}
        self.async_ops = []
        self.dma_hist = {}
        self.recording = None

    def _need(self, op, dep):
        if dep is None or dep is op:
            return
        if dep.eng == "pe" and op.eng == "pe" and not dep.is_dma and not op.is_dma:
            return
        assert dep.sig is not None, "dependency on an op whose signal is not yet defined"
        key, val = dep.sig
        w = self.waited[op.eng]
        if w.get(key, 0) >= val:
            return
        w[key] = val
        op.waits = [(k, v) for (k, v) in op.waits if k != key] + [(key, val)]

    def add(self, eng, fn, r=(), w=(), sig=True, dma=False, async_sem=None):
        if self.recording is not None:
            self.recording.append((eng, fn, tuple(r), tuple(w), sig, dma, async_sem))
            return None
        op = Op(eng, fn, dma)
        psr = [k for k in r if k.startswith("ps")]
        if psr:
            r = [k for k in r if not k.startswith("ps")]
            w = list(w) + [k for k in psr if k not in w]
        for k in r:
            st = self.buf.get(k)
            if st is not None:
                self._need(op, st["w"])
        for k in w:
            st = self.buf.get(k)
            if st is not None:
                self._need(op, st["w"])
                for rd in st["r"].values():
                    self._need(op, rd)
        if async_sem is not None:
            op.sig = (async_sem, 1)
            op.owns = True
            self.async_ops.append(op)
        elif dma:
            n = self.dcount[eng]
            self.dcount[eng] = n + 1
            hist = self.dma_hist.setdefault(eng, [])
            if n >= NDSEM:
                self._need(op, hist[n - NDSEM])
            hist.append(op)
            op.sig = ("d_%s_%d" % (eng, n % NDSEM), 16 * (n // NDSEM + 1))
            op.owns = True
            self.last_dma[eng] = (self.last_dma[eng] + [op])[-NDSEM:]
        elif sig:
            self.count[eng] += 1
            op.sig = ("c_" + eng, self.count[eng])
            op.owns = True
            for p in self.pending_nosig[eng]:
                p.sig = op.sig
            self.pending_nosig[eng] = []
            self.last_sig[eng] = op
        else:
            self.pending_nosig[eng].append(op)
        for k in r:
            st = self.buf.setdefault(k, {"w": None, "r": {}})
            st["r"][(eng, dma)] = op
        for k in w:
            self.buf[k] = {"w": op, "r": {}}
        self.ops[eng].append(op)
        return op

    def barrier(self):
        deps = []
        for e in COMPUTE:
            assert not self.pending_nosig[e]
            if self.last_sig[e] is not None:
                deps.append(self.last_sig[e])
        for e in ENGS:
            deps.extend(self.last_dma[e])
        deps.extend(self.async_ops)
        for e in ENGS:
            op = Op(e, lambda eng: eng.nop(), False)
            for d in deps:
                self._need(op, d)
            if e in COMPUTE:
                self.count[e] += 1
                op.sig = ("c_" + e, self.count[e])
                op.owns = True
                self.last_sig[e] = op
            self.ops[e].append(op)

    def emit(self, nc, stack):
        for e in COMPUTE:
            assert not self.pending_nosig[e], "trailing non-signalling op on " + e
        sems = {}
        for e in COMPUTE:
            sems["c_" + e] = stack.enter_context(nc.semaphore("c_" + e))
        final = {}
        for e in ENGS:
            n = self.dcount[e]
            if n:
                for i in range(NDSEM):
                    k = "d_%s_%d" % (e, i)
                    sems[k] = stack.enter_context(nc.semaphore(k))
                    cnt = (n - i + NDSEM - 1) // NDSEM
                    if cnt:
                        final[k] = 16 * cnt
        for op in self.async_ops:
            k = op.sig[0]
            sems[k] = stack.enter_context(nc.semaphore(k))
            final[k] = 1
        block = stack.enter_context(nc.Block())
        handles = {"pe": block.tensor, "act": block.scalar, "dve": block.vector,
                   "pool": block.gpsimd, "sp": block.sync}

        def make(e):
            def body(eng):
                for op in self.ops[e]:
                    for (k, v) in op.waits:
                        eng.wait_ge(sems[k], v)
                    ins = op.fn(eng)
                    if op.owns:
                        ins.then_inc(sems[op.sig[0]], 16 if op.is_dma else 1)
                if e == "sp":
                    for k, v in final.items():
                        eng.wait_ge(sems[k], v)
                    for ce in COMPUTE:
                        if self.count[ce]:
                            eng.wait_ge(sems["c_" + ce], self.count[ce])
            return body

        for e in ENGS:
            if self.ops[e] or e == "sp":
                handles[e](make(e))


def build_program():
    nc = bass.Bass("TRN2", target_bir_lowering=False)
    try:
        _build(nc)
    except _Stop:
        pass
    return nc


def _build(nc):
    P = Prog()

    def din(name, shape, dt=F32):
        return nc.dram_tensor(name, list(shape), dt, kind="ExternalInput").ap()

    xb_d = din("xb", [T, D])
    cT_d = din("cT", [128, KC])
    wada_d = din("wada", [D, 3072])
    bada_d = din("bada", [1, 3072])
    norms_d = din("norms", [4, D])
    win_d = din("win", [D, WCOLS])
    lbl_d = din("lbl", [128, 4])
    hgn_d = din("hgn", [128, 1])
    gdnn_d = din("gdnn", [128, 1])
    convw_d = din("convw", [128, 24])
    alog_d = din("alog", [1, 2])
    dtb_d = din("dtb", [1, 2])
    if PHASE2:
        wout_d = din("wout", [D, D])
        wff1_d = din("wff1", [D, DFF])
        wff2_d = din("wff2", [DFF, D])
        xtok_d = din("xtok", [T2, D])
    out_d = nc.dram_tensor("out", [T2, D], F32, kind="ExternalOutput").ap()
    dbg = {}
    if DEBUG:
        dbg["proj"] = nc.dram_tensor("dbg_proj", [NCT * 128, STW], F32, kind="ExternalOutput").ap()
        dbg["omix"] = nc.dram_tensor("dbg_omix", [512, T], BF16, kind="ExternalOutput").ap()
        dbg["x1"] = nc.dram_tensor("dbg_x1", [T2, D], F32, kind="ExternalOutput").ap()
        dbg["mod"] = nc.dram_tensor("dbg_mod", [4, 3072], F32, kind="ExternalOutput").ap()

    mod_in = nc.dram_tensor("mod_in", [1, 3072], F32)
    mod_all = nc.dram_tensor("mod_all", [4, 3072], F32)
    oT_loc = nc.dram_tensor("oT_loc", [4 * 512, T2], BF16)
    oT_all = nc.dram_tensor("oT_all", [4 * 4 * 512, T2], BF16)
    x1_d = nc.dram_tensor("x1_scr", [T2, D], F32)
    modflat = mod_all.ap().rearrange("r (o f) -> o (r f)", o=1)

    with ExitStack() as st:
        ARENA = SBUF_BYTES
        arena = st.enter_context(nc.sbuf_tensor("arena", [128, ARENA // 4], F32))
        pos = [0]

        def take(n, dt=F32):
            nbytes = n * (4 if dt == F32 else 2)
            nbytes = (nbytes + 63) // 64 * 64
            a = pos[0]
            assert a + nbytes <= ARENA, "SBUF arena overflow %d" % (a + nbytes)
            pos[0] = a + nbytes
            v = arena[:, a // 4:(a + nbytes) // 4]
            if dt != F32:
                v = v.bitcast(dt)
            return v[:, 0:n]

        psf = [st.enter_context(nc.psum_tensor("psf%d" % i, [128, 512], F32)) for i in range(8)]
        ccsem = [st.enter_context(nc.semaphore("ccsem%d" % i)) for i in range(2)]

        def pf(i):
            return psf[i][:]

        def pb(i):
            return psf[i][:].bitcast(BF16)

        def pk(bank, q0=0, q1=4):
            return ["ps%d" % bank]

        def MM(out, lhsT, rhs, start, stop, r, w, sig=None):
            P.add("pe", lambda e: e.matmul(out, lhsT=lhsT, rhs=rhs, start=start, stop=stop), r=r, w=w,
                  sig=(stop if sig is None else sig))

        def TR(out, in_, ident, r, w, sig=True):
            P.add("pe", lambda e: e.transpose(out=out, in_=in_, identity=ident), r=r, w=w, sig=sig)

        def ACT(out, in_, func, r, w, scale=1.0, bias=None, accum=None):
            def f(e):
                kw = {}
                if bias is not None:
                    kw["bias"] = bias
                if accum is not None:
                    kw["accum_out"] = accum
                return e.activation(out=out, in_=in_, func=func, scale=scale, **kw)
            P.add("act", f, r=r, w=w)

        def TT(eng, out, in0, in1, op, r, w):
            P.add(eng, lambda e: e.tensor_tensor(out=out, in0=in0, in1=in1, op=op), r=r, w=w)

        def TS(eng, out, in0, s1, s2, op0, op1, r, w):
            if s2 is None:
                P.add(eng, lambda e: e.tensor_single_scalar(out=out, in_=in0, scalar=s1, op=op0), r=r, w=w)
            else:
                P.add(eng, lambda e: e.tensor_scalar(out=out, in0=in0, scalar1=s1, scalar2=s2, op0=op0, op1=op1), r=r, w=w)

        def STT(eng, out, in0, scalar, in1, op0, op1, r, w):
            P.add(eng, lambda e: e.scalar_tensor_tensor(out=out, in0=in0, scalar=scalar, in1=in1, op0=op0, op1=op1), r=r, w=w)

        def CP(eng, out, in_, r, w):
            if eng == "act":
                ACT(out, in_, AF.Copy, r, w)
            else:
                P.add(eng, lambda e: e.tensor_copy(out=out, in_=in_), r=r, w=w)

        def MSET(eng, ap, val, w, r=()):
            P.add(eng, lambda e: e.memset(ap, val), r=r, w=w)

        def DMA(eng, out, in_, r, w):
            P.add(eng, lambda e: e.dma_start(out=out, in_=in_), r=r, w=w, dma=True)

        def ASEL(ap, pattern, cmp, fill, base, cm, key):
            P.add("pool", lambda e: e.affine_select(out=ap, in_=ap, pattern=pattern, compare_op=cmp, fill=fill,
                                                    base=base, channel_multiplier=cm), r=[key], w=[key])

        def v3(ap, b):
            return ap.rearrange("p (a b) -> p a b", b=b)

        def ckpt(n):
            if STOP == n:
                P.barrier()
                P.emit(nc, st)
                raise _Stop()

        ident_f = take(128)
        ident_b = take(128, BF16)
        ones_f = take(128)
        onesdiv_b = take(128, BF16)
        onescol_b = take(8, BF16)
        Ublk_f = take(128)
        Bblk_f = take(128)
        C0_f = take(128)
        C1_f = take(128)
        maskU_f = Ublk_f
        mneg_strict = take(128)
        mneg_inclT = take(128)
        rmask = take(STW)
        eps_col = take(1)
        one_col = take(1)
        lnsc_col = take(1)

        MSET("pool", ident_f, 0.0, ["ident_f"])
        ASEL(ident_f, [[-1, 128]], ALU.not_equal, 1.0, 0, 1, "ident_f")
        CP("pool", ident_b, ident_f, ["ident_f"], ["ident_b"])
        MSET("pool", ones_f, 1.0, ["ones_f"])
        MSET("pool", onesdiv_b, 1.0 / 128.0, ["onesdiv_b"])
        MSET("pool", onescol_b, 1.0, ["onescol_b"])
        MSET("pool", Ublk_f, 1.0, ["Ublk"])
        ASEL(Ublk_f, [[1, 128]], ALU.is_ge, 0.0, 0, -1, "Ublk")
        MSET("pool", Ublk_f[0:64, 64:128], 0.0, ["Ublk"], r=["Ublk"])
        MSET("pool", Bblk_f, 0.0, ["Bblk"])
        MSET("pool", Bblk_f[0:64, 0:64], 1.0, ["Bblk"], r=["Bblk"])
        MSET("pool", Bblk_f[64:128, 64:128], 1.0, ["Bblk"], r=["Bblk"])
        MSET("pool", C0_f, 0.0, ["C0"])
        MSET("pool", C0_f[0:64, :], 1.0, ["C0"], r=["C0"])
        MSET("pool", C1_f, 0.0, ["C1"])
        MSET("pool", C1_f[64:128, :], 1.0, ["C1"], r=["C1"])
        MSET("pool", mneg_strict, 0.0, ["mns"])
        ASEL(mneg_strict, [[-1, 128]], ALU.is_gt, NEG, 0, 1, "mns")
        MSET("pool", mneg_strict[64:128, 0:64], NEG, ["mns"], r=["mns"])
        MSET("pool", mneg_inclT, 0.0, ["mni"])
        ASEL(mneg_inclT, [[1, 128]], ALU.is_ge, NEG, 0, -1, "mni")
        MSET("pool", mneg_inclT[0:64, 64:128], NEG, ["mni"], r=["mni"])
        MSET("pool", rmask, 1.0, ["rmask"])
        MSET("pool", v3(rmask, 64)[:, :, 0:1], 0.0, ["rmask"], r=["rmask"])
        maskU4 = take(STW)
        ident4_b = take(STW, BF16)
        for pr in range(4):
            CP("pool", maskU4[:, pr * 128:(pr + 1) * 128], Ublk_f, ["Ublk"], ["maskU4"])
            CP("pool", ident4_b[:, pr * 128:(pr + 1) * 128], ident_f, ["ident_f"], ["ident4_b"])
        MSET("pool", eps_col, EPS, ["eps_col"])
        MSET("pool", one_col, 1.0, ["one_col"])
        MSET("pool", lnsc_col, float(np.log(128.0 ** -0.5)), ["lnsc_col"])

        lbl = take(4)
        hgn = take(1)
        gdnn = take(1)
        convw = take(24)
        alog_b = take(2)
        dtb_b = take(2)
        DMA("sp", lbl, lbl_d[:, :], [], ["lbl"])
        DMA("sp", hgn, hgn_d[:, :], [], ["hgn"])
        DMA("sp", gdnn, gdnn_d[:, :], [], ["gdnn"])
        DMA("sp", convw, convw_d[:, :], [], ["convw"])
        DMA("sp", alog_b, alog_d[0:1, :].partition_broadcast(128), [], ["alog_b"])
        DMA("sp", dtb_b, dtb_d[0:1, :].partition_broadcast(128), [], ["dtb_b"])

        lbt = take(2)
        hg_sc = take(2)
        hg_bi = take(2)
        hg_nsc = take(2)
        TT("dve", lbt, lbl[:, 0:2], lbl[:, 2:4], ALU.subtract, ["lbl"], ["lbt"])
        ACT(lbt, lbt, AF.Tanh, ["lbt"], ["lbt"], scale=0.5)
        TS("dve", hg_sc, lbt, -0.25, 0.25, ALU.mult, ALU.add, ["lbt"], ["hg_sc"])
        TS("dve", hg_bi, lbt, 0.25, 0.75, ALU.mult, ALU.add, ["lbt"], ["hg_bi"])
        TS("dve", hg_nsc, lbt, 0.25, -0.25, ALU.mult, ALU.add, ["lbt"], ["hg_nsc"])
        nA8 = take(8)
        dtb8 = take(8)
        nA2 = take(2)
        ACT(nA2, alog_b, AF.Exp, ["alog_b"], ["nA2"])
        TS("dve", nA2, nA2, -1.0, None, ALU.mult, None, ["nA2"], ["nA2"])
        for pr in range(4):
            CP("dve", nA8[:, 2 * pr:2 * pr + 2], nA2, ["nA2"], ["nA8"])
            CP("dve", dtb8[:, 2 * pr:2 * pr + 2], dtb_b, ["dtb_b"], ["dtb8"])
        dg = take(24 * 128, BF16)
        for i in range(24):
            TS("pool", dg[:, i * 128:(i + 1) * 128], ident_f, convw[:, i:i + 1], None, ALU.mult, None,
               ["ident_f", "convw"], ["dg"])

        ssx = take(2)
        rsx = take(2)
        wmodT = take(KC)
        shT_b = take(KC, BF16)
        pbias = take(NCT)
        pbh = take(2)
        wmod2T = take(KC)
        shfT_b = take(KC, BF16)
        shfT_f = take(KC)
        r16 = take(3 * 128)
        mT = take(3 * KC)
        const_end = pos[0]
        wi = take(KC * WCOLS, BF16)
        wi3 = v3(wi, WCOLS)
        wi_end = pos[0]
        pos[0] = const_end
        woutb = take(KC * D, BF16)
        wout3 = v3(woutb, D)
        pos[0] = wi_end
        hT = take(KC * STW, BF16)
        hT3 = v3(hT, STW)
        xt = [take(D), take(D)]
        hb = take(D, BF16)
        for kc in range(KC):
            DMA("pool", wi3[:, kc, :], win_d[kc * 128:(kc + 1) * 128, :], [], ["wi"])
        base_pos = pos[0]
        ckpt(1)

        cT = take(KC)
        cact = take(KC)
        bada = take(3072)
        modsb = take(3072)
        wa = [take(3072), take(3072)]
        DMA("sp", cT, cT_d[:, :], [], ["cT"])
        DMA("sp", bada[0:1, :], bada_d[0:1, :], [], ["bada"])
        ACT(cact, cT, AF.Silu, ["cT"], ["cact"])
        for kc in range(KC):
            DMA("sp" if kc % 2 == 0 else "act", wa[kc % 2], wada_d[kc * 128:(kc + 1) * 128, :], [], ["wa%d" % (kc % 2)])
            for nt in range(6):
                MM(pf(nt)[0:1, :], cact[:, kc:kc + 1], wa[kc % 2][:, nt * 512:(nt + 1) * 512], kc == 0, kc == KC - 1,
                   ["cact", "wa%d" % (kc % 2)], pk(nt), sig=(nt == 5 or kc == KC - 1))
        for nt in range(6):
            TT("dve", modsb[0:1, nt * 512:(nt + 1) * 512], pf(nt)[0:1, :], bada[0:1, nt * 512:(nt + 1) * 512], ALU.add,
               pk(nt) + ["bada"], ["modsb"])
        DMA("sp", mod_in.ap()[0:1, :], modsb[0:1, :], ["modsb"], ["mod_in"])

        def cc0(e):
            ins = e.collective_compute("AllGather", ALU.bypass, replica_groups=[[0, 1, 2, 3], [4, 5, 6, 7]],
                                       ins=[mod_in.ap().opt()], outs=[mod_all.ap().opt()])
            ins.then_inc(ccsem[0], 1)
            e.wait_ge(ccsem[0], 1)
            return e.nop()
        P.add("pool", cc0, r=["mod_in"], w=["mod_all"])
        if DEBUG:
            DMA("pool", dbg["mod"][:, :], mod_all.ap()[:, :], ["mod_all"], [])
        P.barrier()
        pos[0] = base_pos
        ckpt(2)

        def modT(off_sh, off_sc, nrow, outw, outs_b, tag, outs_f=None):
            r3 = v3(r16, 128)
            DMA("sp", r3[0:16, 0, :], modflat[0:1, off_sh:off_sh + D].rearrange("o (c p) -> (o c) p", p=128), ["mod_all"], ["r16"])
            DMA("sp", r3[0:16, 1, :], modflat[0:1, off_sc:off_sc + D].rearrange("o (c p) -> (o c) p", p=128), ["mod_all"], ["r16"])
            DMA("sp", r3[0:16, 2, :], norms_d[nrow:nrow + 1, :].rearrange("o (c p) -> (o c) p", p=128), [], ["r16"])
            for i in range(3):
                TR(pf(1)[:, i * KC:(i + 1) * KC], r3[0:16, i, :], ident_f[0:16, 0:16], ["r16", "ident_f"], pk(1, 0, 1), sig=(i == 2))
            CP("dve", mT, pf(1)[:, 0:3 * KC], pk(1, 0, 1), ["mT"])
            CP("dve", outs_b, mT[:, 0:KC], ["mT"], [tag + "_s"])
            if outs_f is not None:
                CP("dve", outs_f, mT[:, 0:KC], ["mT"], [tag + "_sf"])
            STT("dve", outw, mT[:, KC:2 * KC], 1.0, mT[:, 2 * KC:3 * KC], ALU.add, ALU.mult, ["mT"], [tag + "_w"])

        modT(0, D, 0, wmodT, shT_b, "m1")
        for ct in range(NCT):
            for kc in range(KC):
                MM(pf(1)[0:cw(ct), 64 + ct:64 + ct + 1], wi3[:, kc, coff(ct):coff(ct) + cw(ct)], shT_b[:, kc:kc + 1], kc == 0, kc == KC - 1,
                   ["wi", "m1_s"], pk(1, 0, 1), sig=(kc == KC - 1 and ct == NCT - 1))
        MSET("pool", pbias, 0.0, ["pbias"])
        CP("dve", pbias[:, 0:8], pf(1)[:, 64:72], pk(1, 0, 1) + ["pbias"], ["pbias"])
        CP("dve", pbias[0:ABW, 8:9], pf(1)[0:ABW, 72:73], pk(1, 0, 1) + ["pbias"], ["pbias"])
        CP("dve", pbias[:, 9:NCT], pf(1)[:, 73:64 + NCT], pk(1, 0, 1) + ["pbias"], ["pbias"])
        TS("dve", pbh, pbias[:, 0:2], 0.5, None, ALU.mult, None, ["pbias"], ["pbh"])
        ckpt(3)

        tk = take
        th = [tk(STW), tk(STW)]
        qraw = [tk(STW), tk(STW)]
        vT = [tk(STW, BF16), tk(STW, BF16)]
        sg = [tk(STW, BF16) for _ in range(4)]
        abrows = tk(STW)
        ug = [[tk(STW + 4, BF16) for _ in range(3)] for _ in range(2)]
        lf = tk(STW)
        kk1 = tk(STW)
        bcs = tk(STW)
        eb = tk(STW)
        enb = tk(STW)
        d2 = tk(STW)
        qh = tk(STW, BF16)
        kt = tk(STW, BF16)
        khT = tk(STW, BF16)
        vtok = tk(STW, BF16)
        khtok = tk(STW, BF16)
        attnm = tk(STW, BF16)
        S32 = [tk(128) for _ in range(4)]
        Sb_hg = [[tk(128, BF16), tk(128, BF16)] for _ in range(2)]
        Sb_gd = [[tk(128, BF16), tk(128, BF16)] for _ in range(2)]
        sqo2 = [tk(STW, BF16), tk(STW, BF16)]
        rso2 = [tk(STW), tk(STW)]
        omix = [tk(STW, BF16), tk(STW, BF16)]
        xs_k = tk(STW, BF16)
        xs_q = tk(STW, BF16)
        xs_v = tk(STW, BF16)
        sqs = tk(STW, BF16)
        abc = tk(16)
        sc = {n: tk(8) for n in ("apd", "e1", "la", "e2", "lnbn", "beta")}
        gsc = tk(32)
        lnr = tk(16)
        hsc = [{n: tk(4) for n in ("aM", "ncS", "aQ", "L3", "skbg", "sktail", "t0", "t1", "tmp")} for _ in range(2)]
        kbg = [tk(STW, BF16), tk(STW, BF16)]
        ktl = [tk(STW, BF16), tk(STW, BF16)]
        vbt = [tk(STW, BF16), tk(STW, BF16)]
        dgc = tk(128)
        dgq = tk(128)
        EN = tk(STW, BF16)
        EQ = tk(STW, BF16)
        qsc = tk(STW, BF16)
        qdec = [tk(STW, BF16), tk(STW, BF16)]
        QKT = [tk(STW, BF16), tk(STW, BF16)]
        Nb = [tk(STW, BF16), tk(STW, BF16)]
        Pb = [tk(STW, BF16), tk(STW, BF16)]
        Rb = [[tk(STW, BF16), tk(STW, BF16)] for _ in range(2)]
        nwT = [tk(STW, BF16), tk(STW, BF16)]
        vnew = [tk(STW, BF16), tk(STW, BF16)]
        p1_end = pos[0]

        for h in range(2):
            MSET("pool", vnew[h], 0.0, ["vnew%d" % h])
        for h in range(4):
            MSET("pool", S32[h], 0.0, ["S32_%d" % h])
        for h in range(2):
            MSET("pool", Sb_hg[h][0], 0.0, ["Sbh%d_0" % h])
            MSET("pool", Sb_gd[h][0], 0.0, ["Sbg%d_0" % h])
            for x in range(3):
                MSET("pool", ug[h][x][:, 0:4], 0.0, ["ug%d%d" % (h, x)])

        PX = 0
        PP = (2, 3)
        PT = 4
        PA, PB_, PC, PD = 5, 6, 7, 1

        def stageA(stt, i):
            tt = stt * 4 + i
            xb_ = xt[tt % 2]
            xk = "xt%d" % (tt % 2)
            hbk = "hb"
            DMA("sp", xb_, xb_d[tt * 128:(tt + 1) * 128, :], [], [xk])
            ACT(hb, xb_, AF.Square, [xk], [hbk, "ssx"], accum=ssx[:, 0:1])
            TS("dve", ssx[:, 1:2], ssx[:, 0:1], 1.0 / D, EPS, ALU.mult, ALU.add, ["ssx"], ["ssx1"])
            ACT(rsx[:, 0:1], ssx[:, 1:2], AF.Ln, ["ssx1"], ["rsx0"])
            ACT(rsx[:, 1:2], rsx[:, 0:1], AF.Exp, ["rsx0"], ["rsx1"], scale=-0.5)
            ACT(hb, xb_, AF.Copy, [xk, "rsx1"], [hbk], scale=rsx[:, 1:2])
            for half in range(2):
                for c8 in range(8):
                    c = half * 8 + c8
                    TR(pb(PX)[:, c8 * 128:(c8 + 1) * 128], hb[:, c * 128:(c + 1) * 128], ident_b,
                       [hbk, "ident_b"], pk(PX), sig=(c8 == 7))
                TT("dve", hT3[:, half * 8:(half + 1) * 8, i * 128:(i + 1) * 128], v3(pb(PX), 128),
                   wmodT[:, half * 8:(half + 1) * 8].unsqueeze(2).to_broadcast([128, 8, 128]), ALU.mult,
                   pk(PX) + ["m1_w"], ["hT"])

        def proj_tile(stt, ct):
            bank = PP[ct % 2]
            for kc in range(KC):
                MM(pf(bank)[0:cw(ct), :], wi3[:, kc, coff(ct):coff(ct) + cw(ct)], hT3[:, kc, :], kc == 0, kc == KC - 1,
                   ["wi", "hT"], pk(bank))
            return pf(bank), pk(bank), pbias[:, ct:ct + 1]

        def EV(eng, out, ps, k, bcol, w):
            if eng == "act":
                ACT(out, ps, AF.Identity, k + ["pbias"], w, bias=bcol)
            else:
                TS(eng, out, ps, bcol, None, ALU.add, None, k + ["pbias"], w)

        def onorm(stt, ps_o, ps_o_key, ncol, ncol_key, sgt, sgk, row0, ty, sbank):
            t0 = stt * STW
            sqo, rso = sqo2[ty], rso2[ty]
            sqk, rsk = "sqo%d" % ty, "rso%d" % ty
            ACT(sqo, ps_o, AF.Square, ps_o_key, [sqk])
            MM(pf(sbank), onesdiv_b, sqo, True, True, ["onesdiv_b", sqk], pk(sbank))
            ACT(rso, pf(sbank), AF.Ln, pk(sbank) + ["eps_col"], [rsk], bias=eps_col[:, 0:1])
            ACT(rso, rso, AF.Exp, [rsk], [rsk], scale=-0.5)
            STT("dve", rso, ps_o, ncol, rso, ALU.mult, ALU.mult, ps_o_key + [rsk, ncol_key], [rsk])
            om = omix[ty]
            omk = "omix%d" % ty
            TT("pool", om, rso, sgt, ALU.mult, [rsk, sgk], [omk])
            jj, toff = t0 // T2, t0 % T2
            DMA("sp", oT_loc.ap()[jj * 512 + row0:jj * 512 + row0 + 128, toff:toff + STW], om, [omk], ["oT_loc%d" % jj])
            if DEBUG:
                DMA("sp", dbg["omix"][row0:row0 + 128, t0:t0 + STW], om, [omk], [])

        def proj_all(stt):
            for h in range(2):
                ps, k, bc = proj_tile(stt, h)
                ACT(th[h], ps, AF.Tanh, k + ["pbh"], ["th%d" % h], scale=0.5, bias=pbh[:, h:h + 1])
                ps, k, bc = proj_tile(stt, 2 + h)
                EV("dve", qraw[h], ps, k, bc, ["qraw%d" % h])
                yield
            for h in range(2):
                ps, k, bc = proj_tile(stt, 4 + h)
                EV("act", vT[h], ps, k, bc, ["vT%d" % h])
                ps, k, bc = proj_tile(stt, 6 + h)
                ACT(sg[h], ps, AF.Silu, k + ["pbias"], ["sg%d" % h], bias=bc)
                yield
            ps, k, bc = proj_tile(stt, 8)
            TS("dve", abrows[0:4, :], ps[0:4, :], pbias[0:4, 8:9], None, ALU.add, None, k + ["pbias"], ["abrows"])
            gdn_scalars(stt)
            for h in range(2):
                for x in range(3):
                    ps, k, bc = proj_tile(stt, 9 + x * 2 + h)
                    EV("act" if x != 1 else "dve", ug[h][x][:, 4:4 + STW], ps, k, bc, ["ug%d%d" % (h, x)])
                    yield
                if h == 0:
                    proj_gate(stt, h)

        def proj_gate(stt, h):
            ps, k, bc = proj_tile(stt, 15 + h)
            ACT(sg[2 + h], ps, AF.Silu, k + ["pbias"], ["sg%d" % (2 + h)], bias=bc)

        def hg_head(stt, h):
            HT, HA, HC = 2, 3, 4
            thh, qrh, vTh = th[h], qraw[h], vT[h]
            thk_, qrk_, vTk_ = "th%d" % h, "qraw%d" % h, "vT%d" % h
            ckpt(61)
            ACT(lf, thh, AF.Ln, [thk_, "hg_sc", "hg_bi"], ["lf"], scale=hg_sc[:, h:h + 1], bias=hg_bi[:, h:h + 1])
            TS("dve", kk1, thh, hg_nsc[:, h:h + 1], hg_sc[:, h:h + 1], ALU.mult, ALU.add, [thk_, "hg_nsc", "hg_sc"], ["kk1"])
            P.add("dve", lambda e: e.tensor_tensor_scan(out=bcs, data0=rmask, data1=lf, initial=0.0,
                                                         op0=ALU.mult, op1=ALU.add), r=["rmask", "lf"], w=["bcs"])
            ACT(eb, bcs, AF.Exp, ["bcs"], ["eb"])
            ACT(enb, bcs, AF.Exp, ["bcs"], ["enb"], scale=-1.0)
            b3 = v3(bcs, 64)
            STT("dve", v3(d2, 64), b3, -1.0, b3[:, :, 63:64].to_broadcast([128, 8, 64]), ALU.mult, ALU.add, ["bcs"], ["d2"])
            ACT(d2, d2, AF.Exp, ["d2"], ["d2"])
            yield
            TT("dve", qh, qrh, eb, ALU.mult, [qrk_, "eb"], ["qh"])
            TT("dve", kt, kk1, enb, ALU.mult, ["kk1", "enb"], ["kt"])
            TT("pool", khT, kk1, d2, ALU.mult, ["kk1", "d2"], ["khT"])
            yield
            ckpt(62)
            for pr in range(4):
                TR(pb(HT)[:, pr * 128:(pr + 1) * 128], vTh[:, pr * 128:(pr + 1) * 128], ident_b, [vTk_, "ident_b"], pk(HT), sig=(pr == 3))
            CP("act", vtok, pb(HT)[:, 0:512], pk(HT), ["vtok"])
            yield
            for pr in range(4):
                TR(pb(HA)[:, pr * 128:(pr + 1) * 128], khT[:, pr * 128:(pr + 1) * 128], ident_b, ["khT", "ident_b"], pk(HA), sig=(pr == 3))
            CP("dve", khtok, pb(HA)[:, 0:512], pk(HA), ["khtok"])
            yield
            ckpt(63)
            for pr in range(4):
                MM(pf(HA)[:, pr * 128:(pr + 1) * 128], kt[:, pr * 128:(pr + 1) * 128], qh[:, pr * 128:(pr + 1) * 128], True, True,
                   ["kt", "qh"], pk(HA), sig=(pr == 3))
            TT("dve", attnm, pf(HA), maskU4, ALU.mult, pk(HA) + ["maskU4"], ["attnm"])
            yield
            ckpt(64)
            Sk = "S32_%d" % h
            for par_, bank in ((0, HT), (1, HA)):
                for pr in range(4):
                    rows = slice(par_ * 64, par_ * 64 + 64)
                    MM(pf(bank)[:, pr * 128:(pr + 1) * 128], khtok[rows, pr * 128:(pr + 1) * 128], vtok[rows, pr * 128:(pr + 1) * 128],
                       True, True, ["khtok", "vtok"], pk(bank), sig=(pr == 3))
            for c in range(8):
                pr, hf = c // 2, c % 2
                rows = slice(hf * 64, hf * 64 + 64)
                cols = slice(c * 64, c * 64 + 64)
                gi = stt * 8 + c
                sap, skey = Sb_hg[h][gi % 2], "Sbh%d_%d" % (h, gi % 2)
                nap, nkey = Sb_hg[h][(gi + 1) % 2], "Sbh%d_%d" % (h, (gi + 1) % 2)
                dbank = HT if hf == 0 else HA
                MM(pf(HC)[:, cols], sap, qh[:, cols], True, False, [skey, "qh"], pk(HC))
                MM(pf(HC)[:, cols], vtok[:, pr * 128:(pr + 1) * 128], attnm[:, pr * 128 + hf * 64:pr * 128 + hf * 64 + 64], False, True,
                   ["vtok", "attnm"], pk(HC), sig=True)
                STT("dve", nap, S32[h], eb[:, c * 64 + 63:c * 64 + 64], pf(dbank)[:, pr * 128:(pr + 1) * 128], ALU.mult, ALU.add,
                    [Sk, "eb"] + pk(dbank), [nkey])
                STT("dve", S32[h], S32[h], eb[:, c * 64 + 63:c * 64 + 64], pf(dbank)[:, pr * 128:(pr + 1) * 128], ALU.mult, ALU.add,
                    [Sk, "eb"] + pk(dbank), [Sk])
                yield
            ckpt(65)
            onorm(stt, pf(HC), pk(HC), hgn[:, 0:1], "hgn", sg[h], "sg%d" % h, h * 128, 0, HA)

        def gdn_scalars(stt):
            for pr in range(4):
                TR(pf(0)[:, pr * 4:pr * 4 + 4], abrows[0:4, pr * 128:(pr + 1) * 128], ident_f[0:4, 0:4], ["abrows", "ident_f"], pk(0), sig=(pr == 3))
            CP("dve", abc, pf(0)[:, 0:16], pk(0), ["abc"])
            a3 = v3(abc, 4)
            TT("dve", v3(sc["apd"], 2), a3[:, :, 0:2], v3(dtb8, 2), ALU.add, ["abc", "dtb8"], ["apd"])
            ACT(sc["e1"], sc["apd"], AF.Exp, ["apd"], ["e1"])
            ACT(sc["e1"], sc["e1"], AF.Ln, ["e1", "one_col"], ["e1"], bias=one_col[:, 0:1])
            TT("dve", sc["la"], sc["e1"], nA8, ALU.mult, ["e1", "nA8"], ["la"])
            ACT(v3(sc["e2"], 2), a3[:, :, 2:4], AF.Exp, ["abc"], ["e2"], scale=-1.0)
            ACT(sc["lnbn"], sc["e2"], AF.Ln, ["e2", "one_col"], ["lnbn"], bias=one_col[:, 0:1])
            ACT(sc["beta"], sc["lnbn"], AF.Exp, ["lnbn"], ["beta"], scale=-1.0)
            for kind, L, Lk in ((0, Ublk_f, "Ublk"), (1, Bblk_f, "Bblk"), (2, C0_f, "C0"), (3, C1_f, "C1")):
                for pr in range(4):
                    MM(pf(0)[:, 32 + kind * 8 + pr * 2:32 + kind * 8 + pr * 2 + 2], L, sc["la"][:, pr * 2:pr * 2 + 2], True, True,
                       [Lk, "la"], pk(0), sig=(kind == 3 and pr == 3))
            CP("dve", gsc, pf(0)[:, 32:64], pk(0), ["gsc"])

        GX, GY, GT = 5, 6, 0
        GO, GV = 7, 1

        def gdn_pre(stt, h):
            hs = hsc[h]
            hk = lambda n: "hsc%d_%s" % (h, n)
            ukeys = ["ug%d%d" % (h, x) for x in range(3)]
            xs = (xs_k, xs_q, xs_v)
            xkeys = ("xsk", "xsq", "xsv")
            kbg_, ktl_, vbt_, qdec_, QKT_, nwT_ = kbg[h], ktl[h], vbt[h], qdec[h], QKT[h], nwT[h]
            kbk, ktk, vbk, qdk, qkk, nwk = ("kbg%d" % h, "ktl%d" % h, "vbt%d" % h, "qdec%d" % h, "QKT%d" % h, "nwT%d" % h)
            Rb_ = Rb[h]
            Rbk = lambda i: "Rb%d_%d" % (h, i)
            for x in range(3):
                ti = x * 2 + h
                for tap in range(4):
                    MM(pf(GX), dg[:, (ti * 4 + tap) * 128:(ti * 4 + tap + 1) * 128], ug[h][x][:, 1 + tap:1 + tap + STW], tap == 0, tap == 3,
                       ["dg", ukeys[x]], pk(GX))
                ACT(xs[x], pf(GX), AF.Silu, pk(GX), [xkeys[x]])
                CP("pool", ug[h][x][:, 1:4], ug[h][x][:, STW + 1:STW + 4], [ukeys[x]], [ukeys[x]])
                yield
            for w_, src, sk in ((0, xs_k, xkeys[0]), (1, xs_q, xkeys[1])):
                TT("pool", sqs, src, src, ALU.mult, [sk], ["sqs"])
                for pr in range(4):
                    MM(pf(GY)[:, 64 + h * 8 + w_ * 4 + pr:64 + h * 8 + w_ * 4 + pr + 1], sqs[:, pr * 128:(pr + 1) * 128], onescol_b[:, 0:1], True, True,
                       ["sqs", "onescol_b"], pk(GY), sig=(pr == 3))
            lr = lnr[:, h * 8:h * 8 + 8]
            lrk = "lnr%d" % h
            ACT(lr, pf(GY)[:, 64 + h * 8:64 + h * 8 + 8], AF.Ln, pk(GY) + ["eps_col"], [lrk], bias=eps_col[:, 0:1])
            TS("dve", lr, lr, -0.5, None, ALU.mult, None, [lrk], [lrk])
            lnrk, lnrq = lr[:, 0:4], lr[:, 4:8]
            g3 = gsc.rearrange("p (k a b) -> p k a b", k=4, b=2)
            g_, gl_, c0_, c1_ = g3[:, 0, :, h], g3[:, 1, :, h], g3[:, 2, :, h], g3[:, 3, :, h]
            lnbn_h = v3(sc["lnbn"], 2)[:, :, h]
            beta_h = v3(sc["beta"], 2)[:, :, h]
            TT("dve", hs["ncS"], lnrk, g_, ALU.subtract, [lrk, "gsc"], [hk("ncS")])
            TT("dve", hs["tmp"], lnrk, g_, ALU.add, [lrk, "gsc"], [hk("tmp")])
            TT("dve", hs["aM"], hs["tmp"], lnbn_h, ALU.subtract, [hk("tmp"), "lnbn"], [hk("aM")])
            STT("dve", hs["aQ"], lnrq, lnsc_col[:, 0:1], g_, ALU.add, ALU.add, [lrk, "gsc", "lnsc_col"], [hk("aQ")])
            TT("dve", hs["L3"], hs["ncS"], gl_, ALU.add, [hk("ncS"), "gsc"], [hk("L3")])
            ACT(hs["skbg"], hs["aM"], AF.Exp, [hk("aM")], [hk("skbg")])
            ACT(hs["sktail"], hs["L3"], AF.Exp, [hk("L3")], [hk("sktail")])
            ACT(hs["t0"], c0_, AF.Exp, ["gsc"], [hk("t0")])
            ACT(hs["t1"], c1_, AF.Exp, ["gsc"], [hk("t1")])
            yield
            for pr in range(4):
                TR(pb(GY)[:, pr * 128:(pr + 1) * 128], xs_k[:, pr * 128:(pr + 1) * 128], ident_b, [xkeys[0], "ident_b"], pk(GY), sig=(pr == 3))
            TT("dve", v3(kbg_, 128), v3(pb(GY)[:, 0:512], 128), hs["skbg"].unsqueeze(2).to_broadcast([128, 4, 128]), ALU.mult,
               pk(GY) + [hk("skbg")], [kbk])
            TT("dve", v3(ktl_, 128), v3(pb(GY)[:, 0:512], 128), hs["sktail"].unsqueeze(2).to_broadcast([128, 4, 128]), ALU.mult,
               pk(GY) + [hk("sktail")], [ktk])
            for pr in range(4):
                TR(pb(GX)[:, pr * 128:(pr + 1) * 128], xs_v[:, pr * 128:(pr + 1) * 128], ident_b, [xkeys[2], "ident_b"], pk(GX), sig=(pr == 3))
            TT("dve", v3(vbt_, 128), v3(pb(GX)[:, 0:512], 128), beta_h.unsqueeze(2).to_broadcast([128, 4, 128]), ALU.mult,
               pk(GX) + ["beta"], [vbk])
            yield
            for pr in range(4):
                sl = slice(pr * 128, (pr + 1) * 128)
                TS("dve", dgc, ident_f, hs["ncS"][:, pr:pr + 1], None, ALU.mult, None, ["ident_f", hk("ncS")], ["dgc"])
                ACT(dgq, ident_f, AF.Copy, ["ident_f", hk("aQ")], ["dgq"], scale=hs["aQ"][:, pr:pr + 1])
                MM(pf(GX)[:, 0:128], ones_f, dgc, True, False, ["ones_f", "dgc"], pk(GX))
                MM(pf(GX)[:, 0:128], ident_f, mneg_strict, False, True, ["ident_f", "mns"], pk(GX))
                ACT(EN[:, sl], pf(GX)[:, 0:128], AF.Exp, pk(GX) + [hk("aM")], ["EN"], bias=hs["aM"][:, pr:pr + 1])
                MM(pf(GY)[:, 0:128], ones_f, dgq, True, True, ["ones_f", "dgq"], pk(GY))
                ACT(qsc[:, sl], pf(GY)[:, 0:128], AF.Exp, pk(GY), ["qsc"])
                MM(pf(GX)[:, 0:128], ones_f, dgq, True, False, ["ones_f", "dgq"], pk(GX))
                MM(pf(GX)[:, 0:128], ident_f, mneg_inclT, False, True, ["ident_f", "mni"], pk(GX))
                ACT(EQ[:, sl], pf(GX)[:, 0:128], AF.Exp, pk(GX) + [hk("ncS")], ["EQ"], bias=hs["ncS"][:, pr:pr + 1])
                yield
            for pr in range(4):
                sl = slice(pr * 128, (pr + 1) * 128)
                MM(pf(GY)[:, sl], xs_k[:, sl], xs_k[:, sl], True, True, [xkeys[0]], pk(GY), sig=(pr == 3))
            STT("dve", Nb[0], pf(GY), -1.0, EN, ALU.mult, ALU.mult, pk(GY) + ["EN"], ["Nb0"])
            for pr in range(4):
                sl = slice(pr * 128, (pr + 1) * 128)
                MM(pf(GX)[:, sl], xs_k[:, sl], xs_q[:, sl], True, True, [xkeys[0], xkeys[1]], pk(GX), sig=(pr == 3))
            TT("dve", QKT_, pf(GX), EQ, ALU.mult, pk(GX) + ["EQ"], [qkk])
            TT("pool", qdec_, xs_q, qsc, ALU.mult, [xkeys[1], "qsc"], [qdk])
            yield
            for pr in range(4):
                sl = slice(pr * 128, (pr + 1) * 128)
                TR(pb(GY)[:, sl], Nb[0][:, sl], ident_b, ["Nb0", "ident_b"], pk(GY), sig=(pr == 3))
            CP("act", Pb[0], pb(GY)[:, 0:512], pk(GY), ["Pb0"])
            TT("pool", Rb_[0], Pb[0], ident4_b, ALU.add, ["Pb0", "ident4_b"], [Rbk(0)])
            for j in range(1, 6):
                a, b_ = (j - 1) % 2, j % 2
                for pr in range(4):
                    sl = slice(pr * 128, (pr + 1) * 128)
                    MM(pf(GX)[:, sl], Pb[a][:, sl], Nb[a][:, sl], True, True, ["Pb%d" % a, "Nb%d" % a], pk(GX), sig=(pr == 3))
                yield
                CP("act", Nb[b_], pf(GX), pk(GX), ["Nb%d" % b_])
                if j < 5:
                    for pr in range(4):
                        sl = slice(pr * 128, (pr + 1) * 128)
                        MM(pf(GY)[:, sl], Nb[a][:, sl], Pb[a][:, sl], True, True, ["Pb%d" % a, "Nb%d" % a], pk(GY), sig=(pr == 3))
                    yield
                    CP("dve", Pb[b_], pf(GY), pk(GY), ["Pb%d" % b_])
                yield
                for pr in range(4):
                    sl = slice(pr * 128, (pr + 1) * 128)
                    MM(pf(GX)[:, sl], Nb[b_][:, sl], Rb_[a][:, sl], True, False, ["Nb%d" % b_, Rbk(a)], pk(GX))
                    MM(pf(GX)[:, sl], ident_b, Rb_[a][:, sl], False, True, ["ident_b", Rbk(a)], pk(GX), sig=(pr == 3))
                CP("act" if j % 2 else "dve", Rb_[b_], pf(GX), pk(GX), [Rbk(b_)])
                yield
            for pr in range(4):
                sl = slice(pr * 128, (pr + 1) * 128)
                MM(pf(GY)[:, sl], kbg_[:, sl], Rb_[1][:, sl], True, True, [kbk, Rbk(1)], pk(GY), sig=(pr == 3))
            ACT(nwT_, pf(GY), AF.Copy, pk(GY), [nwk], scale=-1.0)
            yield

        def gdn_chain(stt, h):
            hs = hsc[h]
            hk = lambda n: "hsc%d_%s" % (h, n)
            ktl_, vbt_, qdec_, QKT_, nwT_, vnew_ = ktl[h], vbt[h], qdec[h], QKT[h], nwT[h], vnew[h]
            ktk, vbk, qdk, qkk, nwk, vnk = ("ktl%d" % h, "vbt%d" % h, "qdec%d" % h, "QKT%d" % h, "nwT%d" % h, "vnew%d" % h)
            R = Rb[h][1]
            Rk = "Rb%d_1" % h
            Sk = "S32_%d" % (2 + h)
            S = S32[2 + h]
            for c in range(8):
                pr, hf = c // 2, c % 2
                sl = slice(pr * 128, (pr + 1) * 128)
                rows = slice(hf * 64, hf * 64 + 64)
                cols = slice(c * 64, c * 64 + 64)
                gi = stt * 8 + c
                Sb = Sb_gd[h][gi % 2]
                Sbk = "Sbg%d_%d" % (h, gi % 2)
                Sbn = Sb_gd[h][(gi + 1) % 2]
                Sbnk = "Sbg%d_%d" % (h, (gi + 1) % 2)
                psV = pf(GV)[:, 0:128]
                psS = pf(GV)[:, 128:256]
                MM(psV, R[:, sl], vbt_[:, sl], True, False, [Rk, vbk], pk(GV))
                MM(psV, nwT_[:, sl], Sb, False, True, [nwk, Sbk], pk(GV))
                yield
                yield
                CP("dve", vnew_[rows, sl], psV[rows, :], pk(GV), [vnk])
                yield
                yield
                MM(pf(GO)[:, cols], Sb, qdec_[:, cols], True, False, [Sbk, qdk], pk(GO))
                MM(pf(GO)[:, cols], vnew_[rows, sl], QKT_[rows, pr * 128 + hf * 64:pr * 128 + hf * 64 + 64], False, True,
                   [vnk, qkk], pk(GO), sig=True)
                MM(psS, ktl_[rows, sl], vnew_[rows, sl], True, True, [ktk, vnk], pk(GV))
                yield
                yield
                tl = hs["t0"] if hf == 0 else hs["t1"]
                STT("dve", Sbn, S, tl[:, pr:pr + 1], psS, ALU.mult, ALU.add, [Sk, hk("t0"), hk("t1")] + pk(GV), [Sbnk])
                STT("dve", S, S, tl[:, pr:pr + 1], psS, ALU.mult, ALU.add, [Sk, hk("t0"), hk("t1")] + pk(GV), [Sk])
                yield
            onorm(stt, pf(GO), pk(GO), gdnn[:, 0:1], "gdnn", sg[2 + h], "sg%d" % (2 + h), (2 + h) * 128, 1, GV)
            yield

        def exchange(jj):
            def cc(e):
                return e.collective_compute("AllGather", ALU.bypass, replica_groups=[[0, 1, 2, 3], [4, 5, 6, 7]],
                                            ins=[oT_loc.ap()[jj * 512:(jj + 1) * 512, :].opt()],
                                            outs=[oT_all.ap()[jj * 2048:(jj + 1) * 2048, :].opt()])
            P.add("pool", cc, r=["oT_loc%d" % jj], w=["oT_all%d" % jj], async_sem="cc_o%d" % jj)

        def interleave(*gens):
            gens = list(gens)
            while gens:
                for g_ in list(gens):
                    try:
                        next(g_)
                    except StopIteration:
                        gens.remove(g_)

        def run(g_):
            for _ in g_:
                pass

        def record(g_):
            P.recording = []
            run(g_)
            ops, P.recording = P.recording, None
            return ops

        def merge(*streams):
            idx = [0] * len(streams)
            for _ in range(sum(len(s_) for s_ in streams)):
                best = min((i for i in range(len(streams)) if idx[i] < len(streams[i])),
                           key=lambda i: (idx[i] + 0.5) / len(streams[i]))
                P.add(*streams[best][idx[best]])
                idx[best] += 1

        def record_calls(*fns):
            P.recording = []
            for f_ in fns:
                f_()
            ops, P.recording = P.recording, None
            return ops

        for i in range(4):
            stageA(0, i)
        run(proj_all(0))
        proj_gate(0, 1)
        if PHASE2:
            modT(3 * D, 4 * D, 2, wmod2T, shfT_b, "m2", shfT_f)
        for stt in range(NST1):
            nxt = stt + 1 < NST1
            sa1 = record_calls(lambda: stageA(stt + 1, 0), lambda: stageA(stt + 1, 1)) if nxt else []
            sa2 = record_calls(lambda: stageA(stt + 1, 2), lambda: stageA(stt + 1, 3)) if nxt else []
            merge(record(gdn_pre(stt, 0)), record(hg_head(stt, 0)), *([sa1] if nxt else []))
            if PHASE2 and stt == NST - 1:
                for c in range(KC):
                    DMA("pool", wout3[:, c, :], wout_d[c * 128:(c + 1) * 128, :], [], ["wi", "woutb"])
            if nxt:
                merge(record(gdn_chain(stt, 0)), record(gdn_pre(stt, 1)), record(hg_head(stt, 1)), sa2)
                merge(record(gdn_chain(stt, 1)), record(proj_all(stt + 1)))
                proj_gate(stt + 1, 1)
            else:
                merge(record(gdn_chain(stt, 0)), record(gdn_pre(stt, 1)))
                merge(record(gdn_chain(stt, 1)), record(hg_head(stt, 1)))
            if stt % 2 == 1 or stt == NST1 - 1:
                exchange(stt // 2)
        P.barrier()

        if not PHASE2:
            DMA("sp", out_d[0:128, :], xt[0], ["xt0"], [])
            P.emit(nc, st)
            return nc
        pos[0] = wi_end
        h2T = take(KC * T2, BF16)
        h2T3 = v3(h2T, T2)
        pb2 = take(64)
        p2mark = pos[0]
        oTs = take(KC * T2, BF16)
        oT3 = v3(oTs, T2)
        A1 = take(D)
        tmpn = take(D)
        xq = [take(D), take(D)]
        t2 = take(512)
        hb2 = [take(D, BF16), take(D, BF16)]
        st2 = take(16)
        h2tmp = take(1024)

        def ld_o(e, r_):
            pid = nc.partition_id([mybir.EngineType.Pool])
            j = pid % 4
            return e.dma_start(out=oT3[:, r_ * 4:(r_ + 1) * 4, :],
                               in_=oT_all.ap()[bass.ds(j * 2048 + r_ * 512, 512), :].rearrange("(b p) t -> p b t", p=128))
        for r_ in range(4):
            P.add("pool", (lambda c_: (lambda e: ld_o(e, c_)))(r_), r=["oT_all%d" % q_ for q_ in range(4)], w=["oTs"], dma=True)
        DMA("sp", A1, modflat[0:1, 2 * D:3 * D].partition_broadcast(128), ["mod_all"], ["A1"])
        DMA("sp", tmpn, norms_d[1:2, :].partition_broadcast(128), [], ["tmpn"])
        TT("dve", A1, A1, tmpn, ALU.mult, ["A1", "tmpn"], ["A1"])

        def rstd_from(ps_banks, keys, outcol, tag, junk):
            for nt in range(4):
                ACT(junk[:, nt * 512:(nt + 1) * 512], ps_banks[nt], AF.Square, keys[nt], ["junkq", "st2_%d" % nt],
                    accum=st2[:, nt:nt + 1])
            TT("dve", st2[:, 4:5], st2[:, 0:1], st2[:, 1:2], ALU.add, ["st2_0", "st2_1"], ["st2_4"])
            TT("dve", st2[:, 5:6], st2[:, 2:3], st2[:, 3:4], ALU.add, ["st2_2", "st2_3"], ["st2_5"])
            TT("dve", st2[:, 4:5], st2[:, 4:5], st2[:, 5:6], ALU.add, ["st2_4", "st2_5"], ["st2_4"])
            TS("dve", st2[:, 6:7], st2[:, 4:5], 1.0 / D, EPS, ALU.mult, ALU.add, ["st2_4"], ["st2_6"])
            ACT(st2[:, 7:8], st2[:, 6:7], AF.Ln, ["st2_6"], ["st2_7"])
            ACT(outcol, st2[:, 7:8], AF.Exp, ["st2_7"], [tag], scale=-0.5)

        def p22_mm(tt):
            xq_ = xq[tt % 2]
            xk = "xq%d" % (tt % 2)
            hbk = "hb2_%d" % (tt % 2)
            hbb = hb2[tt % 2]
            DMA("sp", xq_, xtok_d[tt * 128:(tt + 1) * 128, :], [], [xk])
            for nt in range(4):
                for c in range(KC):
                    MM(pf(nt), oT3[:, c, tt * 128:(tt + 1) * 128], wout3[:, c, nt * 512:(nt + 1) * 512], c == 0, c == KC - 1,
                       ["oTs", "woutb"], pk(nt))

        def p22_norm(tt):
            xq_ = xq[tt % 2]
            xk = "xq%d" % (tt % 2)
            hbk = "hb2_%d" % (tt % 2)
            hbb = hb2[tt % 2]
            for nt in range(4):
                ACT(hbb[:, nt * 512:(nt + 1) * 512], pf(nt), AF.Square, pk(nt), [hbk, "st2_%d" % nt], accum=st2[:, nt:nt + 1])
            TT("dve", st2[:, 4:5], st2[:, 0:1], st2[:, 1:2], ALU.add, ["st2_0", "st2_1"], ["st2_4"])
            TT("dve", st2[:, 5:6], st2[:, 2:3], st2[:, 3:4], ALU.add, ["st2_2", "st2_3"], ["st2_5"])
            TT("dve", st2[:, 4:5], st2[:, 4:5], st2[:, 5:6], ALU.add, ["st2_4", "st2_5"], ["st2_4"])
            TS("dve", st2[:, 6:7], st2[:, 4:5], 1.0 / D, EPS, ALU.mult, ALU.add, ["st2_4"], ["st2_6"])
            ACT(st2[:, 7:8], st2[:, 6:7], AF.Ln, ["st2_6"], ["st2_7"])
            ACT(st2[:, 8:9], st2[:, 7:8], AF.Exp, ["st2_7"], ["rs_y"], scale=-0.5)
            for nt in range(4):
                sl = slice(nt * 512, (nt + 1) * 512)
                STT("dve", t2, pf(nt), st2[:, 8:9], A1[:, sl], ALU.mult, ALU.mult, pk(nt) + ["rs_y", "A1"], ["t2"])
                TT("pool", xq_[:, sl], xq_[:, sl], t2, ALU.add, [xk, "t2"], [xk])
            DMA("sp", x1_d.ap()[tt * 128:(tt + 1) * 128, :], xq_, [xk], ["x1_d"])
            if DEBUG:
                DMA("sp", dbg["x1"][tt * 128:(tt + 1) * 128, :], xq_, [xk], [])
            ACT(hbb, xq_, AF.Square, [xk], [hbk, "st2_9"], accum=st2[:, 9:10])
            TS("dve", st2[:, 10:11], st2[:, 9:10], 1.0 / D, EPS, ALU.mult, ALU.add, ["st2_9"], ["st2_10"])
            ACT(st2[:, 11:12], st2[:, 10:11], AF.Ln, ["st2_10"], ["st2_11"])
            ACT(st2[:, 12:13], st2[:, 11:12], AF.Exp, ["st2_11"], ["rs_x1"], scale=-0.5)
            ACT(hbb, xq_, AF.Copy, [xk, "rs_x1"], [hbk], scale=st2[:, 12:13])

        def p22_tr(tt):
            xq_ = xq[tt % 2]
            xk = "xq%d" % (tt % 2)
            hbk = "hb2_%d" % (tt % 2)
            hbb = hb2[tt % 2]
            for half in range(2):
                for c8 in range(8):
                    c = half * 8 + c8
                    TR(pb(4 + half)[:, c8 * 128:(c8 + 1) * 128], hbb[:, c * 128:(c + 1) * 128], ident_b,
                       [hbk, "ident_b"], pk(4 + half), sig=(c8 == 7))
                TT("dve", v3(h2tmp, 128), v3(pb(4 + half), 128),
                   wmod2T[:, half * 8:(half + 1) * 8].unsqueeze(2).to_broadcast([128, 8, 128]), ALU.mult,
                   pk(4 + half) + ["m2_w"], ["h2tmp"])
                TT("dve", h2T3[:, half * 8:(half + 1) * 8, tt * 128:(tt + 1) * 128], v3(h2tmp, 128),
                   shfT_f[:, half * 8:(half + 1) * 8].unsqueeze(2).to_broadcast([128, 8, 128]), ALU.add,
                   ["h2tmp", "m2_sf"], ["h2T"])

        p22_mm(0)
        for tt in range(8):
            p22_norm(tt)
            if tt + 1 < 8:
                p22_mm(tt + 1)
            p22_tr(tt)
        P.barrier()

        pos[0] = const_end
        acc = take(8 * D)
        acc3 = v3(acc, D)
        pos[0] = p2mark
        G = 4
        NG = 64 // G
        w1r = [take(KC * 128, BF16) for _ in range(2 * G)]
        w2r = [take(G * 512, BF16) for _ in range(4)]
        uT = [take(T2, BF16) for _ in range(2 * G)]
        rl = [take(512), take(512)]
        A2 = take(D)
        x1b = take(D)
        junk3 = take(D, BF16)
        st3 = take(8)
        p23_end = pos[0]
        for g in range(NG):
            for f in range(G):
                fb = g * G + f
                ri = fb % (2 * G)
                DMA("pool", v3(w1r[ri], 128), wff1_d[:, fb * 128:(fb + 1) * 128].rearrange("(c p) f -> p c f", p=128),
                    [], ["w1r%d" % ri])
            for nt in range(4):
                DMA("pool", v3(w2r[nt], 512), wff2_d[g * G * 128:(g + 1) * G * 128, nt * 512:(nt + 1) * 512].rearrange("(f p) n -> p f n", p=128),
                    [], ["w2r%d" % nt])
            for f in range(G):
                fb = g * G + f
                ri = fb % (2 * G)
                for half in range(2):
                    bank = 4 + (fb % 2) * 2 + half
                    for c in range(KC):
                        MM(pf(bank), v3(w1r[ri], 128)[:, c, :], h2T3[:, c, half * 512:(half + 1) * 512], c == 0, c == KC - 1,
                           ["w1r%d" % ri, "h2T"], pk(bank))
                    ACT(rl[half], pf(bank), AF.Relu, pk(bank), ["rl%d" % half])
                    TT("dve" if half == 0 else "pool", uT[ri][:, half * 512:(half + 1) * 512], rl[half], rl[half], ALU.mult,
                       ["rl%d" % half], ["uT%d" % ri])
            last = (g == NG - 1)
            if last:
                DMA("sp", A2, modflat[0:1, 5 * D:6 * D].partition_broadcast(128), ["mod_all"], ["A2"])
                DMA("sp", x1b, norms_d[3:4, :].partition_broadcast(128), [], ["x1b"])
                TT("dve", A2, A2, x1b, ALU.mult, ["A2", "x1b"], ["A2"])
            order = [(nt, tt) for tt in range(8) for nt in range(4)] if last else [(nt, tt) for nt in range(4) for tt in range(8)]
            for i_, (nt, tt) in enumerate(order):
                bank = 1 + i_ % 3
                for f in range(G):
                    ri = (g * G + f) % (2 * G)
                    MM(pf(bank), uT[ri][:, tt * 128:(tt + 1) * 128], v3(w2r[nt], 512)[:, f, :], f == 0, f == G - 1,
                       ["uT%d" % ri, "w2r%d" % nt], pk(bank))
                dst = acc3[:, tt, nt * 512:(nt + 1) * 512]
                ak = "acc%d_%d" % (tt, nt)
                if g == 0:
                    CP("dve", dst, pf(bank), pk(bank), [ak])
                else:
                    TT("dve", dst, pf(bank), dst, ALU.add, pk(bank) + [ak], [ak])
                if last and nt == 3:
                    aks = ["acc%d_%d" % (tt, n_) for n_ in range(4)]
                    DMA("sp", x1b, x1_d.ap()[tt * 128:(tt + 1) * 128, :], ["x1_d"], ["x1b"])
                    ACT(junk3, acc3[:, tt, :], AF.Square, aks, ["junk3", "st3_0"], accum=st3[:, 0:1])
                    TS("dve", st3[:, 1:2], st3[:, 0:1], 1.0 / D, EPS, ALU.mult, ALU.add, ["st3_0"], ["st3_1"])
                    ACT(st3[:, 2:3], st3[:, 1:2], AF.Ln, ["st3_1"], ["st3_2"])
                    ACT(st3[:, 3:4], st3[:, 2:3], AF.Exp, ["st3_2"], ["st3_3"], scale=-0.5)
                    STT("dve", acc3[:, tt, :], acc3[:, tt, :], st3[:, 3:4], A2, ALU.mult, ALU.mult, aks + ["st3_3", "A2"], aks)
                    TT("pool", x1b, x1b, acc3[:, tt, :], ALU.add, ["x1b"] + aks, ["x1b"])
                    DMA("sp", out_d[tt * 128:(tt + 1) * 128, :], x1b, ["x1b"], [])

        print("SBUF use: phase1 %d, phase2.3 %d" % (p1_end, p23_end))
        P.emit(nc, st)
    return nc


_NC_CACHE = {}


def _core_inputs(inp, core):
    b, g = core // 4, core % 4
    f32 = np.float32
    x = inp["x"]
    w_in = inp["w_in"][0]
    hh = [2 * g, 2 * g + 1]
    cols = []
    HGQ, GOFF = 1024, 4096
    def hcols(base, h):
        return np.arange(base + h * 128, base + (h + 1) * 128)
    for base in (HGQ, 0, 2 * HGQ, 3 * HGQ):
        for h in hh:
            cols.append(hcols(base, h))
    abcols = np.array([GOFF + 4096 + hh[0], GOFF + 4096 + hh[1], GOFF + 4096 + 8 + hh[0], GOFF + 4096 + 8 + hh[1]])
    win = np.zeros((D, WCOLS), f32)
    ci = 0
    for cset in cols:
        win[:, coff(ci):coff(ci) + 128] = w_in[:, cset]
        ci += 1
    win[:, coff(ci):coff(ci) + 4] = w_in[:, abcols]
    ci += 1
    for x_, base in enumerate((GOFF + 1024, GOFF, GOFF + 2048)):
        for hi_, h in enumerate(hh):
            ct_ = 9 + 2 * x_ + hi_
            win[:, coff(ct_):coff(ct_) + 128] = w_in[:, hcols(base, h)]
    for hi_, h in enumerate(hh):
        win[:, coff(15 + hi_):coff(15 + hi_) + 128] = w_in[:, hcols(GOFF + 3072, h)]
    lbl = np.zeros((128, 4), f32)
    for layer in range(2):
        for hi, h in enumerate(hh):
            lbl[:, layer * 2 + hi] = inp["hg_lb_logits"][layer, h, :]
    cw = inp["gdn_conv_w"][0]
    convw = np.zeros((128, 24), f32)
    for xi, base in enumerate((1024, 0, 2048)):
        for hi, h in enumerate(hh):
            ti = xi * 2 + hi
            for tap in range(4):
                convw[:, ti * 4 + tap] = cw[tap, base + h * 128: base + (h + 1) * 128]
    return {
        "xb": np.ascontiguousarray(x[b]),
        "xtok": np.ascontiguousarray(x[b, g * T2:(g + 1) * T2]),
        "cT": np.ascontiguousarray(inp["c"][b].reshape(KC, 128).T),
        "wada": np.ascontiguousarray(inp["w_ada"][0][:, g * 3072:(g + 1) * 3072]),
        "bada": np.ascontiguousarray(inp["b_ada"][0][None, g * 3072:(g + 1) * 3072]),
        "win": win,
        "lbl": lbl,
        "convw": convw,
        "alog": np.ascontiguousarray(inp["gdn_a_log"][0][None, hh]),
        "dtb": np.ascontiguousarray(inp["gdn_dt_bias"][0][None, hh]),
    }


def kernel(**inputs):
    inp = {k: np.asarray(v) for k, v in inputs.items()}
    if "nc" not in _NC_CACHE:
        _NC_CACHE["nc"] = build_program()
    nc = _NC_CACHE["nc"]
    f32 = np.float32
    norms = np.ascontiguousarray(np.stack([inp["pre_mix_norm"][0], inp["post_mix_norm"][0],
                                           inp["pre_ffn_norm"][0], inp["post_ffn_norm"][0]]).astype(f32))
    perm = []
    for r in range(4):
        for base in (0, 1024):
            for h in (2 * r, 2 * r + 1):
                perm.append(np.arange(base + h * 128, base + (h + 1) * 128))
    perm = np.concatenate(perm)
    wout = np.ascontiguousarray(inp["w_out"][0][perm])
    shared = {
        "norms": norms,
        "hgn": np.ascontiguousarray(inp["hg_norm"][0][:, None]),
        "gdnn": np.ascontiguousarray(inp["gdn_norm"][0][:, None]),
        "wout": wout,
        "wff1": np.ascontiguousarray(inp["w_ff1"][0]),
        "wff2": np.ascontiguousarray(inp["w_ff2"][0]),
    }
    in_maps = []
    for core in range(8):
        m = _core_inputs(inp, core)
        m.update(shared)
        if not PHASE2:
            for k_ in ("wout", "wff1", "wff2", "xtok"):
                m.pop(k_)
        in_maps.append(m)
    res = run_bass_kernel_spmd(nc, in_maps, core_ids=list(range(8)))
    kernel.last_results = res
    out = np.empty((2, T, D), f32)
    for core in range(8):
        b, g = core // 4, core % 4
        out[b, g * T2:(g + 1) * T2] = res.results[core]["out"]
    return out
```
